# Optimizing a Trainium2 kernel written in Bass

```python
import math
import jax
import jax.numpy as jnp
from jax import lax
import numpy as np

D_MODEL = 1024
BATCH = 2
SEQ = 8192
DEPTH = 2

GRID_W = 64
HEAD_DIM = 64
N_HEADS_TOTAL = D_MODEL // HEAD_DIM
GROUP_HEADS = N_HEADS_TOTAL // 4
GROUP_KV_HEADS = GROUP_HEADS // 2
GQA_GROUP = GROUP_HEADS // GROUP_KV_HEADS
MIX_WIDTH = 4 * GROUP_HEADS * HEAD_DIM
ROPE_THETA = 10000.0
NORM_EPS = 1e-6
Q_BLOCK = 128
NEG_INF = -1e30

NA_WIN_ROWS = 8
NA_WIN_COLS = 16

MLA_Q_RANK = D_MODEL // 4
MLA_KV_RANK = D_MODEL // 8
MLA_NOPE = HEAD_DIM
MLA_ROPE = HEAD_DIM // 2
MLA_V = HEAD_DIM

SW_WINDOW = 128

D_FF = 2816
CONV_W = 3

SPLIT_SIZES = (
    GROUP_HEADS * HEAD_DIM, GROUP_HEADS * HEAD_DIM, GROUP_HEADS * HEAD_DIM,
    MLA_Q_RANK, MLA_KV_RANK, MLA_ROPE,
    GROUP_HEADS * HEAD_DIM, GROUP_KV_HEADS * HEAD_DIM, GROUP_KV_HEADS * HEAD_DIM,
    GROUP_HEADS * HEAD_DIM, GROUP_KV_HEADS * HEAD_DIM, GROUP_KV_HEADS * HEAD_DIM,
)
IN_COLS = sum(SPLIT_SIZES)

kernel_name = 'hybrid_parallel_head_group_encoder'


def rms_norm(x, gain):
    xf = x.astype(jnp.float32)
    y = xf * lax.rsqrt(jnp.mean(xf * xf, axis=-1, keepdims=True) + NORM_EPS) * gain.astype(jnp.float32)
    return y.astype(x.dtype)


def rope_angles(pos, dim):
    inv = ROPE_THETA ** (-jnp.arange(0, dim, 2, dtype=jnp.float32) / dim)
    return pos.astype(jnp.float32)[:, None] * inv[None, :]


def apply_rope(x, ang):
    cos = jnp.cos(ang)[None, :, None, :]
    sin = jnp.sin(ang)[None, :, None, :]
    x1, x2 = jnp.split(x.astype(jnp.float32), 2, axis=-1)
    out = jnp.concatenate([x1 * cos - x2 * sin, x2 * cos + x1 * sin], axis=-1)
    return out.astype(x.dtype)


def axial_rope(x, ang_row, ang_col):
    half = x.shape[-1] // 2
    return jnp.concatenate([apply_rope(x[..., :half], ang_row), apply_rope(x[..., half:], ang_col)], axis=-1)


def split_columns(z):
    points = []
    acc = 0
    for n in SPLIT_SIZES[:-1]:
        acc += n
        points.append(acc)
    return jnp.split(z, points, axis=-1)


def dense_attention_blocked(q, k, v, scale):
    b, s, hkv, g, dq = q.shape
    nb = s // Q_BLOCK
    qb = q.reshape(b, nb, Q_BLOCK, hkv, g, dq).transpose(1, 0, 2, 3, 4, 5)

    def one_block(q_blk):
        sc = jnp.einsum('bqhgd,bkhd->bhgqk', q_blk, k).astype(jnp.float32) * scale
        p = jax.nn.softmax(sc, axis=-1).astype(v.dtype)
        return jnp.einsum('bhgqk,bkhd->bqhgd', p, v)

    out = lax.map(one_block, qb)
    return out.transpose(1, 0, 2, 3, 4, 5).reshape(b, s, hkv, g, v.shape[-1])


def sliding_window_sink_attention(q, k, v, sink, scale):
    b, s, hkv, g, d = q.shape
    nb = s // Q_BLOCK
    pad = ((0, 0), (Q_BLOCK, Q_BLOCK), (0, 0), (0, 0))
    kp = jnp.pad(k, pad).reshape(b, nb + 2, Q_BLOCK, hkv, d)
    vp = jnp.pad(v, pad).reshape(b, nb + 2, Q_BLOCK, hkv, d)
    kb = jnp.concatenate([kp[:, :-2], kp[:, 1:-1], kp[:, 2:]], axis=2)
    vb = jnp.concatenate([vp[:, :-2], vp[:, 1:-1], vp[:, 2:]], axis=2)
    qb = q.reshape(b, nb, Q_BLOCK, hkv, g, d)
    sc = jnp.einsum('bnqhgd,bnkhd->bnhgqk', qb, kb).astype(jnp.float32) * scale
    blk = jnp.arange(nb)[:, None] * Q_BLOCK
    qpos = blk + jnp.arange(Q_BLOCK)[None, :]
    kpos = blk - Q_BLOCK + jnp.arange(3 * Q_BLOCK)[None, :]
    valid = (jnp.abs(qpos[:, :, None] - kpos[:, None, :]) <= SW_WINDOW) & ((kpos >= 0) & (kpos < s))[:, None, :]
    sc = jnp.where(valid[None, :, None, None], sc, NEG_INF)
    sink_l = jnp.broadcast_to(sink.reshape(hkv, g).astype(jnp.float32)[None, None, :, :, None, None],
                              sc.shape[:-1] + (1,))
    p = jax.nn.softmax(jnp.concatenate([sc, sink_l], axis=-1), axis=-1)[..., :-1]
    out = jnp.einsum('bnhgqk,bnkhd->bnqhgd', p.astype(v.dtype), vb)
    return out.reshape(b, s, hkv, g, d)


def neighbourhood_attention_2d(q, k, v, rpb):
    b, s, h, d = q.shape
    rows = s // GRID_W
    kr = min(NA_WIN_ROWS, rows)
    kc = NA_WIN_COLS
    qg = q.reshape(b, rows, GRID_W, h, d)
    kg = k.reshape(b, rows, GRID_W, h, d)
    vg = v.reshape(b, rows, GRID_W, h, d)
    col = jnp.arange(GRID_W)
    col_start = jnp.clip(col - kc // 2, 0, GRID_W - kc)
    col_idx = col_start[:, None] + jnp.arange(kc)[None, :]
    col_off = col_idx - col[:, None] + (NA_WIN_COLS - 1)
    row_ids = jnp.arange(rows)
    row_start = jnp.clip(row_ids - kr // 2, 0, rows - kr)
    scale = HEAD_DIM ** -0.5

    def one_row(args):
        r, rs = args
        q_r = lax.dynamic_index_in_dim(qg, r, axis=1, keepdims=False)
        k_band = lax.dynamic_slice_in_dim(kg, rs, kr, axis=1)
        v_band = lax.dynamic_slice_in_dim(vg, rs, kr, axis=1)
        k_nb = k_band[:, :, col_idx]
        v_nb = v_band[:, :, col_idx]
        row_off = rs + jnp.arange(kr) - r + (NA_WIN_ROWS - 1)
        bias = rpb[:, row_off[:, None, None], col_off[None, :, :]]
        sc = jnp.einsum('bchd,bacwhd->bhcaw', q_r, k_nb).astype(jnp.float32) * scale
        sc = sc + bias.transpose(0, 2, 1, 3)[None].astype(jnp.float32)
        p = jax.nn.softmax(sc.reshape(b, h, GRID_W, kr * kc), axis=-1)
        p = p.reshape(b, h, GRID_W, kr, kc).astype(v.dtype)
        return jnp.einsum('bhcaw,bacwhd->bchd', p, v_nb)

    out = lax.map(one_row, (row_ids, row_start))
    return out.transpose(1, 0, 2, 3, 4).reshape(b, s, h, d)


def conv_gated_mlp(h, w_up, conv_w, conv_b, w_down):
    u = h @ w_up
    s = u.shape[1]
    half = CONV_W // 2
    up = jnp.pad(u, ((0, 0), (half, half), (0, 0)))
    c = conv_b
    for i in range(CONV_W):
        c = c + up[:, i:i + s] * conv_w[i]
    gate, val = jnp.split(c, 2, axis=-1)
    return (jax.nn.gelu(gate, approximate=True) * val) @ w_down


def setup_inputs(seed: int = 0) -> dict:
    key = jax.random.key(seed)
    ks = jax.random.split(key, 20)
    f32 = jnp.float32

    def dense(k, shape, fan_in):
        return jax.random.normal(k, shape, f32) * fan_in ** -0.5

    def gain(k, n):
        return 1.0 + 0.05 * jax.random.normal(k, (DEPTH, n), f32)

    return {
        'x': jax.random.normal(ks[0], (BATCH, SEQ, D_MODEL), f32),
        'mix_pre_gain': gain(ks[1], D_MODEL),
        'w_in': dense(ks[2], (DEPTH, D_MODEL, IN_COLS), D_MODEL),
        'na_rpb': 0.1 * jax.random.normal(ks[3], (DEPTH, GROUP_HEADS, 2 * NA_WIN_ROWS - 1, 2 * NA_WIN_COLS - 1), f32),
        'mla_q_gain': gain(ks[4], MLA_Q_RANK),
        'mla_w_uq': dense(ks[5], (DEPTH, MLA_Q_RANK, GROUP_HEADS * (MLA_NOPE + MLA_ROPE)), MLA_Q_RANK),
        'mla_kv_gain': gain(ks[6], MLA_KV_RANK),
        'mla_w_ukv': dense(ks[7], (DEPTH, MLA_KV_RANK, GROUP_HEADS * (MLA_NOPE + MLA_V)), MLA_KV_RANK),
        'ax_q_gain': gain(ks[8], HEAD_DIM),
        'ax_k_gain': gain(ks[9], HEAD_DIM),
        'sw_sink': 0.5 * jax.random.normal(ks[10], (DEPTH, GROUP_HEADS), f32),
        'w_out': dense(ks[11], (DEPTH, MIX_WIDTH, D_MODEL), MIX_WIDTH),
        'mix_post_gain': gain(ks[12], D_MODEL),
        'ffn_pre_gain': gain(ks[13], D_MODEL),
        'w_up': dense(ks[14], (DEPTH, D_MODEL, 2 * D_FF), D_MODEL),
        'conv_w': dense(ks[15], (DEPTH, CONV_W, 2 * D_FF), CONV_W),
        'conv_b': 0.01 * jax.random.normal(ks[16], (DEPTH, 2 * D_FF), f32),
        'w_down': dense(ks[17], (DEPTH, D_FF, D_MODEL), D_FF),
        'ffn_post_gain': gain(ks[18], D_MODEL),
    }


def reference(x, mix_pre_gain, w_in, na_rpb, mla_q_gain, mla_w_uq, mla_kv_gain, mla_w_ukv,
              ax_q_gain, ax_k_gain, sw_sink, w_out, mix_post_gain, ffn_pre_gain, w_up,
              conv_w, conv_b, w_down, ffn_post_gain):
    b, s, _ = x.shape
    t = jnp.arange(s)
    ang_full = rope_angles(t, HEAD_DIM)
    ang_mla = rope_angles(t, MLA_ROPE)
    ang_row = rope_angles(t // GRID_W, HEAD_DIM // 2)
    ang_col = rope_angles(t % GRID_W, HEAD_DIM // 2)

    def heads(z, n):
        return z.reshape(b, s, n, -1)

    for l in range(DEPTH):
        h = rms_norm(x, mix_pre_gain[l])
        (a_q, a_k, a_v, b_cq, b_ckv, b_kr,
         c_q, c_k, c_v, d_q, d_k, d_v) = split_columns(h @ w_in[l])

        o_a = neighbourhood_attention_2d(heads(a_q, GROUP_HEADS), heads(a_k, GROUP_HEADS),
                                         heads(a_v, GROUP_HEADS), na_rpb[l])

        q_b = (rms_norm(b_cq, mla_q_gain[l]) @ mla_w_uq[l]).reshape(b, s, GROUP_HEADS, MLA_NOPE + MLA_ROPE)
        q_nope, q_pe = jnp.split(q_b, [MLA_NOPE], axis=-1)
        q_b = jnp.concatenate([q_nope, apply_rope(q_pe, ang_mla)], axis=-1)
        kv_b = (rms_norm(b_ckv, mla_kv_gain[l]) @ mla_w_ukv[l]).reshape(b, s, GROUP_HEADS, MLA_NOPE + MLA_V)
        k_nope, v_b = jnp.split(kv_b, [MLA_NOPE], axis=-1)
        k_pe = apply_rope(b_kr.reshape(b, s, 1, MLA_ROPE), ang_mla)
        k_b = jnp.concatenate([k_nope, jnp.broadcast_to(k_pe, (b, s, GROUP_HEADS, MLA_ROPE))], axis=-1)
        o_b = dense_attention_blocked(q_b[:, :, :, None, :], k_b, v_b, (MLA_NOPE + MLA_ROPE) ** -0.5)

        q_c = axial_rope(rms_norm(heads(c_q, GROUP_HEADS), ax_q_gain[l]), ang_row, ang_col)
        k_c = axial_rope(rms_norm(heads(c_k, GROUP_KV_HEADS), ax_k_gain[l]), ang_row, ang_col)
        o_c = dense_attention_blocked(q_c.reshape(b, s, GROUP_KV_HEADS, GQA_GROUP, HEAD_DIM), k_c,
                                      heads(c_v, GROUP_KV_HEADS), HEAD_DIM ** -0.5)

        q_d = apply_rope(heads(d_q, GROUP_HEADS), ang_full).reshape(b, s, GROUP_KV_HEADS, GQA_GROUP, HEAD_DIM)
        k_d = apply_rope(heads(d_k, GROUP_KV_HEADS), ang_full)
        o_d = sliding_window_sink_attention(q_d, k_d, heads(d_v, GROUP_KV_HEADS), sw_sink[l], HEAD_DIM ** -0.5)

        mixed = jnp.concatenate([o_a.reshape(b, s, -1), o_b.reshape(b, s, -1),
                                 o_c.reshape(b, s, -1), o_d.reshape(b, s, -1)], axis=-1) @ w_out[l]
        x = x + rms_norm(mixed, mix_post_gain[l])

        h = rms_norm(x, ffn_pre_gain[l])
        y = conv_gated_mlp(h, w_up[l], conv_w[l], conv_b[l], w_down[l])
        x = x + rms_norm(y, ffn_post_gain[l])
    return x
```

```python
import os
import time
import numpy as np
import ml_dtypes
from contextlib import ExitStack
import concourse.bass as bass
import concourse.mybir as mybir
from concourse.bass_utils import run_bass_kernel_spmd

F32 = mybir.dt.float32
BF16 = mybir.dt.bfloat16
ALU = mybir.AluOpType
AF = mybir.ActivationFunctionType
NPBF = ml_dtypes.bfloat16

NCORES = 8
DM = 1024
SEQ = 8192
TCORE = 2048
NBLK = TCORE // 512
EPS = 1e-6
DFF = 2816


class Sched:
    ENGS = ("pe", "act", "dve", "pool", "sp")

    def __init__(self, nc):
        self.nc = nc
        self.ops = {e: [] for e in self.ENGS}
        self.cnt = {e: 0 for e in self.ENGS}
        self.lastw = {}
        self.readers = {}
        self.waited = {e: {} for e in self.ENGS}
        self.dma_cnt = {}
        self.semkeys = []

    def _deps(self, eng, reads, writes):
        toks = []
        for r in reads:
            if r in self.lastw:
                toks.append(self.lastw[r])
        for w in writes:
            if w in self.lastw:
                toks.append(self.lastw[w])
            toks.extend(self.readers.get(w, ()))
        need = {}
        for k, v in toks:
            if eng == "pe" and k == "E_pe":
                continue
            if v > need.get(k, 0):
                need[k] = v
        waits = []
        for k, v in need.items():
            if self.waited[eng].get(k, 0) < v:
                self.waited[eng][k] = v
                waits.append((k, v))
        return waits

    def _commit(self, tok, reads, writes):
        for r in reads:
            self.readers.setdefault(r, []).append(tok)
        for w in writes:
            self.lastw[w] = tok
            self.readers[w] = []

    def op(self, eng, fn, reads=(), writes=()):
        waits = self._deps(eng, reads, writes)
        self.cnt[eng] += 1
        key = "E_" + eng
        if key not in self.semkeys:
            self.semkeys.append(key)
        tok = (key, self.cnt[eng])
        self._commit(tok, reads, writes)
        self.ops[eng].append((waits, fn, key, 1))
        return tok

    def dma(self, eng, fn, reads=(), writes=(), semkey=None):
        waits = self._deps(eng, reads, writes)
        if semkey is None:
            semkey = "D_" + str(writes[0] if writes else reads[0])
        if semkey not in self.semkeys:
            self.semkeys.append(semkey)
        self.dma_cnt[semkey] = self.dma_cnt.get(semkey, 0) + 16
        tok = (semkey, self.dma_cnt[semkey])
        self._commit(tok, reads, writes)
        self.ops[eng].append((waits, fn, semkey, 16))
        return tok

    def coll(self, eng, fn, reads=(), writes=(), semkey=None):
        waits = self._deps(eng, reads, writes)
        if semkey is None:
            semkey = "C_" + str(writes[0])
        if semkey not in self.semkeys:
            self.semkeys.append(semkey)
        self.dma_cnt[semkey] = self.dma_cnt.get(semkey, 0) + 1
        tok = (semkey, self.dma_cnt[semkey])
        self._commit(tok, reads, writes)
        self.ops[eng].append((waits, fn, semkey, 1))
        return tok

    def emit(self, final_eng="sp"):
        nc = self.nc
        with ExitStack() as st:
            sems = {}
            for i, k in enumerate(self.semkeys):
                sems[k] = st.enter_context(nc.semaphore("s%d" % i))
            final_waits = [(k, v) for k, v in self.dma_cnt.items()]
            final_waits += [("E_" + e, self.cnt[e]) for e in self.ENGS if self.cnt[e]]
            block = st.enter_context(nc.Block())
            reg = {"pe": block.tensor, "act": block.scalar, "dve": block.vector,
                   "pool": block.gpsimd, "sp": block.sync}
            for e in self.ENGS:
                oplist = self.ops[e]
                fw = final_waits if e == final_eng else []
                if not oplist and not fw:
                    continue

                def body(eng, oplist=oplist, fw=fw):
                    for waits, fn, key, inc in oplist:
                        for k, v in waits:
                            eng.wait_ge(sems[k], v)
                        ins = fn(eng)
                        ins.then_inc(sems[key], inc)
                    for k, v in fw:
                        eng.wait_ge(sems[k], v)
                reg[e](body)
        return len(self.semkeys)


class Rot:
    def __init__(self, items):
        self.items = items
        self.i = 0

    def next(self):
        it = self.items[self.i % len(self.items)]
        self.i += 1
        return it


def _mk_consts(nc, S, sb):
    identf = sb("identf", [128, 128], F32)
    ident = sb("ident", [128, 128], BF16)
    S.op("pool", lambda e: e.memset(identf[:], 0.0), writes=["identf"])
    S.op("pool", lambda e: e.affine_select(out=identf[:], in_=identf[:], pattern=[[-1, 128]],
                                           compare_op=ALU.not_equal, fill=1.0, base=0,
                                           channel_multiplier=1),
         reads=["identf"], writes=["identf"])
    S.op("dve", lambda e: e.tensor_copy(out=ident[:], in_=identf[:]), reads=["identf"], writes=["ident"])
    return ident


cAq, cAk, cCq, cCqp, cCk, cCkp, cDq, cDqp, cDk, cDkp = 0, 256, 512, 768, 1024, 1152, 1280, 1536, 1792, 1920
cBcq, cBckv, cBkr, cBkrp, cV, NCX = 2048, 2304, 2432, 2528, 2624, 3136
SC64 = 0.125
SC96 = 96 ** -0.5


def build_P():
    nc = bass.Bass("TRN2", target_bir_lowering=False)

    def din(name, shape, dt=F32):
        return nc.dram_tensor(name, shape, dt, kind="ExternalInput").ap()

    def dout(name, shape, dt=BF16):
        return nc.dram_tensor(name, shape, dt, kind="ExternalOutput").ap()

    x = din("x", [TCORE, DM])
    gpre = din("gpre", [1, DM])
    wext = din("wext", [DM, NCX])
    wuq = din("wuq", [256, 768])
    wukv = din("wukv", [128, 512])
    tab = din("tab", [6, 128, TCORE])
    pcol = din("pcol", [128, 8])
    o_qA = dout("qA", [256, TCORE]); o_kA = dout("kA", [256, TCORE])
    o_qC = dout("qC", [256, TCORE]); o_kC = dout("kC", [128, TCORE])
    o_qD = dout("qD", [256, TCORE]); o_kD = dout("kD", [128, TCORE])
    o_qB = dout("qB", [384, TCORE]); o_kB = dout("kB", [384, TCORE])
    o_V = dout("V", [TCORE, 512]); o_vB = dout("vB", [TCORE, 256])

    with ExitStack() as st:
        def sb(n, s, d):
            return st.enter_context(nc.sbuf_tensor(n, s, d))

        def ps(n, s, d=F32):
            return st.enter_context(nc.psum_tensor(n, s, d))

        S = Sched(nc)
        W = sb("W", [128, 8, NCX], BF16)
        WQ = sb("WQ", [128, 2, 768], BF16)
        WKV = sb("WKV", [128, 512], BF16)
        gp = sb("gp", [128, DM], F32)
        pc = sb("pc", [128, 8], F32)
        ones = sb("ones", [128, 128], BF16)
        onesblk = sb("onesblk", [128, 128], BF16)
        eps1 = sb("eps1", [128, 1], F32)
        eps64 = sb("eps64", [128, 1], F32)
        xt = [sb("xt%d" % i, [128, DM], F32) for i in range(2)]
        xsq = sb("xsq", [128, DM], BF16)
        ssq = [sb("ssq%d" % i, [128, 1], F32) for i in range(2)]
        rstd = [sb("rstd%d" % i, [128, 1], F32) for i in range(2)]
        hb = [sb("hb%d" % i, [128, DM], BF16) for i in range(2)]
        hT = [sb("hT%d" % i, [128, 8, 512], BF16) for i in range(2)]
        tabs = sb("tabs", [128, 6, 512], F32)
        qA_s = [sb("qA_s%d" % i, [128, 2, 512], BF16) for i in range(2)]
        kA_s = [sb("kA_s%d" % i, [128, 2, 512], BF16) for i in range(2)]
        qC_s = [sb("qC_s%d" % i, [128, 2, 512], BF16) for i in range(2)]
        kC_s = [sb("kC_s%d" % i, [128, 1, 512], BF16) for i in range(2)]
        qD_s = [sb("qD_s%d" % i, [128, 2, 512], BF16) for i in range(2)]
        kD_s = [sb("kD_s%d" % i, [128, 1, 512], BF16) for i in range(2)]
        qB_s = [sb("qB_s%d" % i, [96, 4, 512], BF16) for i in range(2)]
        kB_s = [sb("kB_s%d" % i, [96, 4, 512], BF16) for i in range(2)]
        V_s = [sb("V_s%d" % i, [128, 4, 512], BF16) for i in range(2)]
        vB_s = [sb("vB_s%d" % i, [128, 4, 256], BF16) for i in range(2)]
        sqb = Rot([(sb("sqb%d" % i, [128, 512], BF16), "sqb%d" % i) for i in range(3)])
        rsb = Rot([(sb("rsb%d" % i, [128, 512], F32), "rsb%d" % i) for i in range(2)])
        t1b = Rot([(sb("t1b%d" % i, [128, 512], F32), "t1b%d" % i) for i in range(3)])
        t2b = Rot([(sb("t2b%d" % i, [128, 512], F32), "t2b%d" % i) for i in range(3)])
        cqn = sb("cqn", [128, 2, 512], BF16)
        ckvn = sb("ckvn", [128, 512], BF16)
        pt = ps("pt", [128, DM], BF16)
        pj = Rot([(ps("pj%d" % i, [128, 512]), "pj%d" % i) for i in range(5)])
        pn = ps("pn", [128, 512])
        pv = ps("pv", [128, 512])

        ident = _mk_consts(nc, S, sb)
        S.op("pool", lambda e: e.memset(ones[:], 1.0), writes=["ones"])
        S.op("pool", lambda e: e.memset(onesblk[:], 0.0), writes=["onesblk"])
        S.op("pool", lambda e: e.memset(onesblk[0:64, 0:64], 1.0), reads=["onesblk"], writes=["onesblk"])
        S.op("pool", lambda e: e.memset(onesblk[64:128, 64:128], 1.0), reads=["onesblk"], writes=["onesblk"])
        S.op("pool", lambda e: e.memset(eps1[:], EPS), writes=["eps1"])
        S.op("pool", lambda e: e.memset(eps64[:], 64 * EPS), writes=["eps64"])
        S.dma("sp", lambda e: e.dma_start(out=gp[:], in_=gpre.partition_broadcast(128)), writes=["gp"])
        S.dma("sp", lambda e: e.dma_start(out=pc[:], in_=pcol), writes=["pc"])
        wv = wext.rearrange("(c p) n -> p c n", p=128)
        WG = ((0, 512), (1280, 2048), (512, 1280), (2048, 2624), (2624, NCX))
        for gi, (c0, c1) in enumerate(WG):
            S.dma("pool", lambda e, c0=c0, c1=c1: e.dma_start(out=W[:, :, c0:c1], in_=wv[:, :, c0:c1]),
                  writes=["W%d" % gi], semkey="D_W%d" % gi)

        def wres(col):
            for gi, (c0, c1) in enumerate(WG):
                if c0 <= col < c1:
                    return "W%d" % gi
        S.dma("pool", lambda e: e.dma_start(out=WQ[:], in_=wuq.rearrange("(c p) n -> p c n", p=128)),
              writes=["WQ"])
        S.dma("pool", lambda e: e.dma_start(out=WKV[:], in_=wukv), writes=["WKV"])

        def proj(pap, col0, M, par):
            def fn(e):
                for c in range(8):
                    ins = e.matmul(pap[0:M, :], lhsT=W[:, c, col0:col0 + M], rhs=hT[par][:, c, :],
                                   start=(c == 0), stop=(c == 7))
                return ins
            return fn

        def step1(tb):
            par = tb % 2
            hTr = ["hT%d_%d" % (par, j) for j in range(4)]
            for j in range(4):
                i = tb * 4 + j
                b2 = i % 2
                S.dma("sp", lambda e, i=i, b2=b2: e.dma_start(out=xt[b2][:], in_=x[i * 128:(i + 1) * 128, :]),
                      writes=["xt%d" % b2])
                S.op("pool", lambda e, b2=b2: e.memset(ssq[b2][:], 0.0), writes=["ssq%d" % b2])
                S.op("act", lambda e, b2=b2: e.activation(out=xsq[:], in_=xt[b2][:], func=AF.Square,
                                                          accum_out=ssq[b2][:]),
                     reads=["xt%d" % b2, "ssq%d" % b2], writes=["xsq", "ssq%d" % b2])
                S.op("act", lambda e, b2=b2: e.activation(out=rstd[b2][:], in_=ssq[b2][:], func=AF.Sqrt,
                                                          bias=eps1[:], scale=1.0 / DM),
                     reads=["ssq%d" % b2, "eps1"], writes=["rstd%d" % b2])
                S.op("dve", lambda e, b2=b2: e.reciprocal(out=rstd[b2][:], in_=rstd[b2][:]),
                     reads=["rstd%d" % b2], writes=["rstd%d" % b2])
                S.op("dve", lambda e, b2=b2: e.scalar_tensor_tensor(
                    out=hb[b2][:], in0=xt[b2][:], scalar=rstd[b2][:, 0:1], in1=gp[:], op0=ALU.mult, op1=ALU.mult),
                    reads=["xt%d" % b2, "rstd%d" % b2, "gp"], writes=["hb%d" % b2])

                def tr(e, b2=b2):
                    for c in range(8):
                        ins = e.transpose(out=pt[:, c * 128:(c + 1) * 128], in_=hb[b2][:, c * 128:(c + 1) * 128],
                                          identity=ident[:])
                    return ins
                S.op("pe", tr, reads=["hb%d" % b2, "ident"], writes=["pt"])
                S.op("act", lambda e, par=par, j=j: e.copy(out=hT[par][:, :, j * 128:(j + 1) * 128],
                                                           in_=pt[:].rearrange("p (c t) -> p c t", c=8)),
                     reads=["pt"], writes=[hTr[j]])

        def jobs(tb):
            par = tb % 2
            hTr = ["hT%d_%d" % (par, j) for j in range(4)]
            S.dma("sp", lambda e, tb=tb: e.dma_start(
                out=tabs[:], in_=tab.rearrange("k p t -> p k t")[:, :, tb * 512:(tb + 1) * 512]),
                writes=["tabs"])

            def rWc(col):
                return [wres(col)] + hTr

            for (col, dst, dname, sc) in ((cAq, qA_s, "qA_s", SC64), (cAk, kA_s, "kA_s", 1.0)):
                for rb in range(2):
                    p, pnm = pj.next()
                    S.op("pe", proj(p, col + rb * 128, 128, par), reads=rWc(col), writes=[pnm])
                    S.op("act", lambda e, p=p, dst=dst, rb=rb, sc=sc, par=par: e.mul(
                        out=dst[par][:, rb, :], in_=p[:], mul=sc), reads=[pnm], writes=["%s%d" % (dname, par)])
            for (col, colp, nrb, dst, dname, sc) in ((cDq, cDqp, 2, qD_s, "qD_s", SC64), (cDk, cDkp, 1, kD_s, "kD_s", 1.0)):
                for rb in range(nrb):
                    pm, pmn = pj.next()
                    pp, ppn = pj.next()
                    t1, t1n = t1b.next()
                    t2, t2n = t2b.next()
                    S.op("pe", proj(pm, col + rb * 128, 128, par), reads=rWc(col), writes=[pmn])
                    S.op("pe", proj(pp, colp + rb * 128, 128, par), reads=rWc(colp), writes=[ppn])
                    S.op("dve", lambda e, pm=pm, t1=t1, sc=sc: e.scalar_tensor_tensor(
                        out=t1[:], in0=pm[:], scalar=sc, in1=tabs[:, 0, :], op0=ALU.mult, op1=ALU.mult),
                        reads=[pmn, "tabs"], writes=[t1n])
                    S.op("dve", lambda e, pp=pp, t2=t2, sc=sc: e.scalar_tensor_tensor(
                        out=t2[:], in0=pp[:], scalar=sc, in1=tabs[:, 1, :], op0=ALU.mult, op1=ALU.mult),
                        reads=[ppn, "tabs"], writes=[t2n])
                    S.op("pool", lambda e, t1=t1, t2=t2, dst=dst, rb=rb, par=par: e.tensor_tensor(
                        out=dst[par][:, rb, :], in0=t1[:], in1=t2[:], op=ALU.add),
                        reads=[t1n, t2n], writes=["%s%d" % (dname, par)])
            for (col, colp, nrb, dst, dname, g0, epsb, epsn, nsc) in (
                    (cCq, cCqp, 2, qC_s, "qC_s", 0, eps64, "eps64", 1.0),
                    (cCk, cCkp, 1, kC_s, "kC_s", 2, eps1, "eps1", 1.0 / 64)):
                for rb in range(nrb):
                    pm, pmn = pj.next()
                    pp, ppn = pj.next()
                    t1, t1n = t1b.next()
                    t2, t2n = t2b.next()
                    sq, sqn = sqb.next()
                    rs, rsn = rsb.next()
                    S.op("pe", proj(pm, col + rb * 128, 128, par), reads=rWc(col), writes=[pmn])
                    S.op("pe", proj(pp, colp + rb * 128, 128, par), reads=rWc(colp), writes=[ppn])
                    S.op("act", lambda e, pm=pm, sq=sq: e.activation(out=sq[:], in_=pm[:], func=AF.Square),
                         reads=[pmn], writes=[sqn])
                    S.op("pe", lambda e, sq=sq: e.matmul(pn[:], lhsT=onesblk[:], rhs=sq[:], start=True, stop=True),
                         reads=[sqn, "onesblk"], writes=["pn"])
                    S.op("act", lambda e, rs=rs, epsb=epsb, nsc=nsc: e.activation(
                        out=rs[:], in_=pn[:], func=AF.Sqrt, bias=epsb[:], scale=nsc),
                        reads=["pn", epsn], writes=[rsn])
                    S.op("dve", lambda e, rs=rs: e.reciprocal(out=rs[:], in_=rs[:]), reads=[rsn], writes=[rsn])
                    S.op("dve", lambda e, pm=pm, t1=t1, g0=g0: e.scalar_tensor_tensor(
                        out=t1[:], in0=pm[:], scalar=pc[:, g0:g0 + 1], in1=tabs[:, 2, :], op0=ALU.mult, op1=ALU.mult),
                        reads=[pmn, "tabs", "pc"], writes=[t1n])
                    S.op("dve", lambda e, pp=pp, t2=t2, g0=g0: e.scalar_tensor_tensor(
                        out=t2[:], in0=pp[:], scalar=pc[:, g0 + 1:g0 + 2], in1=tabs[:, 3, :], op0=ALU.mult, op1=ALU.mult),
                        reads=[ppn, "tabs", "pc"], writes=[t2n])
                    S.op("pool", lambda e, t1=t1, t2=t2: e.tensor_tensor(out=t1[:], in0=t1[:], in1=t2[:], op=ALU.add),
                         reads=[t1n, t2n], writes=[t1n])
                    S.op("pool", lambda e, t1=t1, rs=rs, dst=dst, rb=rb, par=par: e.tensor_tensor(
                        out=dst[par][:, rb, :], in0=t1[:], in1=rs[:], op=ALU.mult),
                        reads=[t1n, rsn], writes=["%s%d" % (dname, par)])
            pm0, pm0n = pj.next()
            pm1, pm1n = pj.next()
            sq0, sq0n = sqb.next()
            sq1, sq1n = sqb.next()
            rs, rsn = rsb.next()
            S.op("pe", proj(pm0, cBcq, 128, par), reads=rWc(cBcq), writes=[pm0n])
            S.op("pe", proj(pm1, cBcq + 128, 128, par), reads=rWc(cBcq), writes=[pm1n])
            S.op("act", lambda e, pm0=pm0, sq0=sq0: e.activation(out=sq0[:], in_=pm0[:], func=AF.Square),
                 reads=[pm0n], writes=[sq0n])
            S.op("act", lambda e, pm1=pm1, sq1=sq1: e.activation(out=sq1[:], in_=pm1[:], func=AF.Square),
                 reads=[pm1n], writes=[sq1n])

            def nrm2(e, sq0=sq0, sq1=sq1):
                e.matmul(pn[:], lhsT=ones[:], rhs=sq0[:], start=True, stop=False)
                return e.matmul(pn[:], lhsT=ones[:], rhs=sq1[:], start=False, stop=True)
            S.op("pe", nrm2, reads=[sq0n, sq1n, "ones"], writes=["pn"])
            S.op("act", lambda e, rs=rs: e.activation(out=rs[:], in_=pn[:], func=AF.Sqrt, bias=eps1[:], scale=1.0 / 256),
                 reads=["pn", "eps1"], writes=[rsn])
            S.op("dve", lambda e, rs=rs: e.reciprocal(out=rs[:], in_=rs[:]), reads=[rsn], writes=[rsn])
            for c, (pm, pmn) in enumerate(((pm0, pm0n), (pm1, pm1n))):
                S.op("dve", lambda e, pm=pm, c=c, rs=rs: e.scalar_tensor_tensor(
                    out=cqn[:, c, :], in0=pm[:], scalar=pc[:, 4 + c:5 + c], in1=rs[:], op0=ALU.mult, op1=ALU.mult),
                    reads=[pmn, rsn, "pc"], writes=["cqn"])
            for h in range(4):
                pm, pmn = pj.next()
                pp, ppn = pj.next()
                t1, t1n = t1b.next()
                t2, t2n = t2b.next()

                def up(pap, c0):
                    def fn(e):
                        e.matmul(pap[0:96, :], lhsT=WQ[:, 0, c0:c0 + 96], rhs=cqn[:, 0, :], start=True, stop=False)
                        return e.matmul(pap[0:96, :], lhsT=WQ[:, 1, c0:c0 + 96], rhs=cqn[:, 1, :], start=False, stop=True)
                    return fn
                S.op("pe", up(pm, h * 192), reads=["WQ", "cqn"], writes=[pmn])
                S.op("pe", up(pp, h * 192 + 96), reads=["WQ", "cqn"], writes=[ppn])
                S.op("act", lambda e, pm=pm, h=h, par=par: e.mul(out=qB_s[par][0:64, h, :], in_=pm[0:64, :], mul=SC96),
                     reads=[pmn], writes=["qB_s%d" % par])
                S.op("dve", lambda e, pm=pm, t1=t1: e.scalar_tensor_tensor(
                    out=t1[64:96, :], in0=pm[64:96, :], scalar=SC96, in1=tabs[64:96, 4, :], op0=ALU.mult, op1=ALU.mult),
                    reads=[pmn, "tabs"], writes=[t1n])
                S.op("dve", lambda e, pp=pp, t2=t2: e.scalar_tensor_tensor(
                    out=t2[64:96, :], in0=pp[64:96, :], scalar=SC96, in1=tabs[64:96, 5, :], op0=ALU.mult, op1=ALU.mult),
                    reads=[ppn, "tabs"], writes=[t2n])
                S.op("pool", lambda e, t1=t1, t2=t2, h=h, par=par: e.tensor_tensor(
                    out=qB_s[par][64:96, h, :], in0=t1[64:96, :], in1=t2[64:96, :], op=ALU.add),
                    reads=[t1n, t2n], writes=["qB_s%d" % par])
            pm, pmn = pj.next()
            sq, sqn = sqb.next()
            rs, rsn = rsb.next()
            S.op("pe", proj(pm, cBckv, 128, par), reads=rWc(cBckv), writes=[pmn])
            S.op("act", lambda e, pm=pm, sq=sq: e.activation(out=sq[:], in_=pm[:], func=AF.Square),
                 reads=[pmn], writes=[sqn])
            S.op("pe", lambda e, sq=sq: e.matmul(pn[:], lhsT=ones[:], rhs=sq[:], start=True, stop=True),
                 reads=[sqn, "ones"], writes=["pn"])
            S.op("act", lambda e, rs=rs: e.activation(out=rs[:], in_=pn[:], func=AF.Sqrt, bias=eps1[:], scale=1.0 / 128),
                 reads=["pn", "eps1"], writes=[rsn])
            S.op("dve", lambda e, rs=rs: e.reciprocal(out=rs[:], in_=rs[:]), reads=[rsn], writes=[rsn])
            S.op("dve", lambda e, pm=pm, rs=rs: e.scalar_tensor_tensor(
                out=ckvn[:], in0=pm[:], scalar=pc[:, 6:7], in1=rs[:], op0=ALU.mult, op1=ALU.mult),
                reads=[pmn, rsn, "pc"], writes=["ckvn"])
            for h in range(4):
                pu, pun = pj.next()
                S.op("pe", lambda e, pu=pu, h=h: e.matmul(pu[0:64, :], lhsT=WKV[:, h * 64:(h + 1) * 64], rhs=ckvn[:],
                                                          start=True, stop=True),
                     reads=["WKV", "ckvn"], writes=[pun])
                S.op("act", lambda e, pu=pu, h=h, par=par: e.copy(out=kB_s[par][0:64, h, :], in_=pu[0:64, :]),
                     reads=[pun], writes=["kB_s%d" % par])
            for j in range(4):
                S.op("pe", lambda e, j=j: e.matmul(pv[:, 0:256], lhsT=ckvn[:, j * 128:(j + 1) * 128], rhs=WKV[:, 256:512],
                                                   start=True, stop=True),
                     reads=["WKV", "ckvn"], writes=["pv"])
                S.op("act", lambda e, j=j, par=par: e.copy(out=vB_s[par][:, j, :], in_=pv[:, 0:256]),
                     reads=["pv"], writes=["vB_s%d" % par])
            pm, pmn = pj.next()
            pp, ppn = pj.next()
            t1, t1n = t1b.next()
            t2, t2n = t2b.next()
            S.op("pe", proj(pm, cBkr, 96, par), reads=rWc(cBkr), writes=[pmn])
            S.op("pe", proj(pp, cBkrp, 96, par), reads=rWc(cBkrp), writes=[ppn])
            S.op("dve", lambda e, pm=pm, t1=t1: e.tensor_tensor(out=t1[64:96, :], in0=pm[64:96, :], in1=tabs[64:96, 4, :],
                                                                op=ALU.mult), reads=[pmn, "tabs"], writes=[t1n])
            S.op("dve", lambda e, pp=pp, t2=t2: e.tensor_tensor(out=t2[64:96, :], in0=pp[64:96, :], in1=tabs[64:96, 5, :],
                                                                op=ALU.mult), reads=[ppn, "tabs"], writes=[t2n])
            for h in range(4):
                S.op("pool", lambda e, t1=t1, t2=t2, h=h, par=par: e.tensor_tensor(
                    out=kB_s[par][64:96, h, :], in0=t1[64:96, :], in1=t2[64:96, :], op=ALU.add),
                    reads=[t1n, t2n], writes=["kB_s%d" % par])
            for j in range(4):
                def vmm(e, j=j, par=par):
                    for c in range(8):
                        ins = e.matmul(pv[:], lhsT=hT[par][:, c, j * 128:(j + 1) * 128], rhs=W[:, c, cV:cV + 512],
                                       start=(c == 0), stop=(c == 7))
                    return ins
                S.op("pe", vmm, reads=rWc(cV), writes=["pv"])
                S.op("dve", lambda e, j=j, par=par: e.tensor_copy(out=V_s[par][:, j, :], in_=pv[:]),
                     reads=["pv"], writes=["V_s%d" % par])
            tsl = slice(tb * 512, (tb + 1) * 512)
            for (dr, src, nm, rr) in ((o_qA, qA_s, "qA_s", 128), (o_kA, kA_s, "kA_s", 128), (o_qC, qC_s, "qC_s", 128),
                                      (o_kC, kC_s, "kC_s", 128), (o_qD, qD_s, "qD_s", 128), (o_kD, kD_s, "kD_s", 128),
                                      (o_qB, qB_s, "qB_s", 96), (o_kB, kB_s, "kB_s", 96)):
                S.dma("sp", lambda e, dr=dr, src=src, rr=rr, par=par, tsl=tsl: e.dma_start(
                    out=dr.rearrange("(j p) t -> p j t", p=rr)[:, :, tsl], in_=src[par][:]),
                    reads=["%s%d" % (nm, par)], writes=["o_%s_%d" % (nm, tb)], semkey="D_st_%s%d" % (nm, par))
            S.dma("sp", lambda e, par=par, tb=tb: e.dma_start(
                out=o_V[tb * 512:(tb + 1) * 512, :].rearrange("(j p) n -> p j n", p=128), in_=V_s[par][:]),
                reads=["V_s%d" % par], writes=["o_V_%d" % tb], semkey="D_st_V%d" % par)
            S.dma("sp", lambda e, par=par, tb=tb: e.dma_start(
                out=o_vB[tb * 512:(tb + 1) * 512, :].rearrange("(j p) n -> p j n", p=128), in_=vB_s[par][:]),
                reads=["vB_s%d" % par], writes=["o_vB_%d" % tb], semkey="D_st_vB%d" % par)
        step1(0)
        for tb in range(NBLK):
            if tb + 1 < NBLK:
                step1(tb + 1)
            jobs(tb)
        S.emit()
    return nc


_SPLITS = (256, 256, 256, 256, 128, 32, 256, 128, 128, 256, 128, 128)
_OFF = np.concatenate([[0], np.cumsum(_SPLITS)])


def _perm_full():
    d = np.arange(64)
    return np.where(d < 32, d + 32, d - 32)


def _perm_ax():
    d = np.arange(64)
    return np.where((d % 32) < 16, d + 16, d - 16)


def _perm32():
    d = np.arange(32)
    return np.where(d < 16, d + 16, d - 16)


def _wext_index():
    o = _OFF
    rng = lambda i: np.arange(o[i], o[i + 1])

    def permheads(i, nh, perm):
        return np.concatenate([o[i] + h * 64 + perm for h in range(nh)])
    pad64 = -np.ones(64, np.int64)
    idx = np.concatenate([
        rng(0), rng(1),
        rng(6), permheads(6, 4, _perm_ax()), rng(7), permheads(7, 2, _perm_ax()),
        rng(9), permheads(9, 4, _perm_full()), rng(10), permheads(10, 2, _perm_full()),
        rng(3), rng(4),
        pad64, rng(5), pad64, o[5] + _perm32(),
        rng(2), rng(8), rng(11)])
    assert idx.shape[0] == NCX
    return idx


def _gather_cols(w, idx):
    out = np.zeros((w.shape[0], idx.shape[0]), w.dtype)
    m = idx >= 0
    out[:, m] = w[:, idx[m]]
    return out


def _rope_tables(r0):
    t = (r0 + np.arange(TCORE)).astype(np.float32)

    def ang(pos, dim):
        inv = (np.float32(10000.0) ** (-(np.arange(0, dim, 2, dtype=np.float32) / np.float32(dim)))).astype(np.float32)
        return (pos[:, None] * inv[None, :]).astype(np.float32)
    a_full = ang(t, 64)
    a_mla = ang(t, 32)
    ti = r0 + np.arange(TCORE)
    a_row = ang((ti // 64).astype(np.float32), 32)
    a_col = ang((ti % 64).astype(np.float32), 32)
    tabs = np.zeros((6, 128, TCORE), np.float32)
    for p in range(128):
        d = p % 64
        a = a_full[:, d % 32]
        tabs[0, p] = np.cos(a)
        tabs[1, p] = np.sin(a) * (-1.0 if d < 32 else 1.0)
        if d < 32:
            a = a_row[:, d % 16]
            sg = -1.0 if d < 16 else 1.0
        else:
            a = a_col[:, (d - 32) % 16]
            sg = -1.0 if (d - 32) < 16 else 1.0
        tabs[2, p] = np.cos(a)
        tabs[3, p] = np.sin(a) * sg
        if 64 <= p < 96:
            jj = p - 64
            a = a_mla[:, jj % 16]
            tabs[4, p] = np.cos(a)
            tabs[5, p] = np.sin(a) * (-1.0 if jj < 16 else 1.0)
    return tabs


_CACHE = {}
_TRACE_KW = {}


def _get(name, builder):
    if name not in _CACHE:
        _CACHE[name] = builder()
    return _CACHE[name]


def run_P(xfull, l, inp):
    nc = _get("P", build_P)
    widx = _get("widx", _wext_index)
    wext = _gather_cols(inp["w_in"][l], widx)
    wuq_src = inp["mla_w_uq"][l]
    cols = []
    for h in range(4):
        base = h * 96
        cols.append(np.arange(base, base + 96))
        cols.append(np.concatenate([-np.ones(64, np.int64), base + 64 + _perm32()]))
    wuq = _gather_cols(wuq_src, np.concatenate(cols))
    kvidx = np.concatenate([np.concatenate([np.arange(h * 128, h * 128 + 64) for h in range(4)]),
                            np.concatenate([np.arange(h * 128 + 64, h * 128 + 128) for h in range(4)])])
    wukv = _gather_cols(inp["mla_w_ukv"][l], kvidx)
    pa = _perm_ax()
    p64 = np.arange(128) % 64
    pcol = np.zeros((128, 8), np.float32)
    pcol[:, 0] = inp["ax_q_gain"][l][p64]
    pcol[:, 1] = inp["ax_q_gain"][l][pa[p64]]
    pcol[:, 2] = inp["ax_k_gain"][l][p64]
    pcol[:, 3] = inp["ax_k_gain"][l][pa[p64]]
    pcol[:, 4] = inp["mla_q_gain"][l][0:128]
    pcol[:, 5] = inp["mla_q_gain"][l][128:256]
    pcol[:, 6] = inp["mla_kv_gain"][l]
    gpre = np.ascontiguousarray(inp["mix_pre_gain"][l][None, :])
    in_maps = []
    for c in range(NCORES):
        b, r0 = c // 4, (c % 4) * TCORE
        tabs = _get("tab%d" % r0, lambda r0=r0: _rope_tables(r0))
        in_maps.append(dict(x=np.ascontiguousarray(xfull[b, r0:r0 + TCORE]), gpre=gpre, wext=wext, wuq=wuq,
                            wukv=wukv, tab=tabs, pcol=pcol))
    t0 = time.time()
    res = run_bass_kernel_spmd(nc, in_maps, core_ids=list(range(NCORES)), **_TRACE_KW)
    print("[kernel] launch took %.1fs" % (time.time() - t0), flush=True)
    return res.results


ARN = 23552
NWIN = 22
A_SLOT = [0, 1] + [2] * 12 + [3, 4]
D_SLOT = [0] + [1] * 14 + [2]
NEG = -1e30


def build_T():
    nc = bass.Bass("TRN2", target_bir_lowering=False)

    def din(name, shape, dt=F32):
        return nc.dram_tensor(name, shape, dt, kind="ExternalInput").ap()

    KD = din("KD", [4, 128, 2 * SEQ], BF16)
    VD = din("VD", [4, 128, 64 * 256], BF16)
    QD = din("QD", [4, 128, 2 * TCORE], BF16)
    kAw = din("kAw", [128, 2 * NWIN * 128], BF16)
    vAw = din("vAw", [128, NWIN * 256], BF16)
    kDw = din("kDw", [128, NWIN * 128], BF16)
    vDw = din("vDw", [128, NWIN * 128], BF16)
    qAd = din("qAd", [128, 4 * TCORE], BF16)
    qDd = din("qDd", [128, 4 * TCORE], BF16)
    tabA = din("tabA", [5, 128, 7 * 512], F32)
    tabD = din("tabD", [3, 128, 3 * 512], BF16)
    sinkc = din("sinkc", [128, 2], F32)
    xin = din("x", [TCORE, DM])
    wout = din("wout", [DM, DM])
    gpost = din("gpost", [1, DM])
    o_xm = nc.dram_tensor("xm", [TCORE, DM], F32, kind="ExternalOutput").ap()
    o_mix = nc.dram_tensor("mixT", [128, 8 * TCORE], BF16, kind="ExternalOutput").ap()

    with ExitStack() as st:
        def sb(n, s, d):
            return st.enter_context(nc.sbuf_tensor(n, s, d))

        def ps(n, s, d=F32):
            return st.enter_context(nc.psum_tensor(n, s, d))

        S = Sched(nc)
        AR = [sb("AR%d" % i, [128, ARN], BF16) for i in range(2)]
        mixT = sb("mixT_s", [128, 8, TCORE], BF16)
        VB = sb("VB", [128, 64, 2, 128], BF16)
        PTb = [sb("PTp%d" % i, [128, 2, 512], BF16) for i in range(4)]
        PTP = Rot([(PTb[i], "PT%d_0" % i, "PT%d_1" % i) for i in range(4)])
        PT = Rot([(PTb[i][:, j, :], "PT%d_%d" % (i, j)) for i in range(4) for j in range(2)])
        ones = sb("ones", [128, 64], BF16)
        rec = [sb("rec%d" % i, [128, 512], F32) for i in range(2)]
        sinke = sb("sinke", [128, 2], F32)
        gpo = sb("gpo", [128, DM], F32)
        eps1 = sb("eps1", [128, 1], F32)
        xt = [sb("xt%d" % i, [128, DM], F32) for i in range(2)]
        yt = [sb("yt%d" % i, [128, DM], F32) for i in range(2)]
        junk = sb("junk", [128, 512], BF16)
        ssq = [sb("ssq%d" % i, [128, 2], F32) for i in range(2)]
        rstd = [sb("rstd%d" % i, [128, 1], F32) for i in range(2)]
        psS = ps("psS", [128, 4, 512])
        acc = [(ps("acc%d_o" % i, [128, 512]), ps("acc%d_s" % i, [128, 512])) for i in range(2)]

        ident = _mk_consts(nc, S, sb)
        S.op("pool", lambda e: e.memset(ones[:], 1.0), writes=["ones"])
        S.op("pool", lambda e: e.memset(eps1[:], EPS), writes=["eps1"])
        S.dma("sp", lambda e: e.dma_start(out=sinke[:], in_=sinkc), writes=["sinke"])
        S.op("act", lambda e: e.activation(out=sinke[:], in_=sinke[:], func=AF.Exp), reads=["sinke"], writes=["sinke"])
        S.dma("sp", lambda e: e.dma_start(out=gpo[:], in_=gpost.partition_broadcast(128)), writes=["gpo"])

        def dense_views(s):
            a = AR[s]
            return (a[:, 0:16384].rearrange("p (u t) -> p u t", u=2), VB,
                    a[:, 16384:20480].rearrange("p (u t) -> p u t", u=2))

        def v_aug(s, kt, u):
            return VB[:, kt, u, :]

        def load_v(ph):
            S.dma("sp", lambda e, ph=ph: e.dma_start(
                out=VB[:], in_=VD[ph].rearrange("p (t u n) -> p t u n", u=2, n=128)), writes=["sV"], semkey="D_V")

        def load_dense(ph):
            s = ph % 2
            K, V, Q = dense_views(s)
            nm = ["s%dK" % s, "sV", "s%dQ" % s]
            for u in range(2):
                S.dma("sp", lambda e, K=K, ph=ph, u=u: e.dma_start(
                    out=K[:, u, :], in_=KD[ph][:, u * SEQ:(u + 1) * SEQ]), writes=[nm[0]], semkey="D_K%d" % s)
            S.dma("sp", lambda e, Q=Q, ph=ph: e.dma_start(
                out=Q[:, :, :], in_=QD[ph].rearrange("p (u t) -> p u t", u=2)), writes=[nm[2]], semkey="D_Q%d" % s)

        NPH = int(os.environ.get("T_NPH", "4"))
        DO_AD = int(os.environ.get("T_AD", "1"))
        DO_WO = int(os.environ.get("T_WO", "1"))
        pairs = [(ph, qg, kt) for ph in range(NPH) for qg in range(4) for kt in range(64)]
        NP_ = len(pairs)
        LOOKP = 1
        pendp = {}
        sbank = 0

        def load_ad():
            a = AR[0]
            al = ["s0K", "s0Q"]
            v = dict(kA=a[:, 0:5632], vA=a[:, 5632:11264], qA=a[:, 11264:19456])
            for nm, src in (("kA", kAw), ("vA", vAw), ("qA", qAd)):
                S.dma("sp", lambda e, dst=v[nm], src=src: e.dma_start(out=dst, in_=src),
                      writes=al + ["ad_" + nm], semkey="D_ad_" + nm)

        def load_d():
            a = AR[1]
            v = dict(kD=a[:, 8192:11008], vD=a[:, 11008:13824], qD=a[:, 13824:22016])
            for nm, src in (("kD", kDw), ("vD", vDw), ("qD", qDd)):
                S.dma("sp", lambda e, dst=v[nm], src=src: e.dma_start(out=dst, in_=src),
                      writes=["s1K", "s1Q", "ad_" + nm], semkey="D_ad_" + nm)

        def load_wo():
            WO = AR[1][:, 0:8192].rearrange("p (c n) -> p c n", c=8)
            wv = wout.rearrange("(c p) n -> p c n", p=128)
            for c in range(8):
                S.dma("pool", lambda e, c=c: e.dma_start(out=WO[:, c, :], in_=wv[:, c, :]),
                      writes=(["s1K", "s1Q", "WO"] if c == 0 else ["WO"]), semkey="D_WO")

        if not DO_AD or NPH < 4:
            S.op("pool", lambda e: e.memset(mixT[:], 0.0),
                 writes=(["mixT_%d_%d_%d" % (c, q, u) for c in range(2, 6) for q in range(4) for u in range(2)] +
                         ["mixT_%s_%d" % (k, i) for k in "AD" for i in range(16)]))
        if NPH > 0:
            load_dense(0)
            load_v(0)
        if NPH > 1:
            load_dense(1)
        if NPH < 4:
            load_ad()
        for n in range(NP_ + LOOKP):
            if n < NP_:
                ph, qg, kt = pairs[n]
                s = ph % 2
                K, V, Q = dense_views(s)
                b2 = (n % 2) * 2
                ptp, ptn0, ptn1 = PTP.next()

                def smm2(e, K=K, Q=Q, b2=b2, kt=kt, qg=qg):
                    for u in range(2):
                        ins = e.matmul(psS[:, b2 + u, :], lhsT=K[:, u, kt * 128:(kt + 1) * 128],
                                       rhs=Q[:, u, qg * 512:(qg + 1) * 512], start=True, stop=True)
                    return ins
                S.op("pe", smm2, reads=["s%dK" % s, "s%dQ" % s], writes=["psS%d" % b2, "psS%d" % (b2 + 1)])
                S.op("act", lambda e, b2=b2, ptp=ptp: e.activation(out=ptp[:], in_=psS[:, b2:b2 + 2, :], func=AF.Exp),
                     reads=["psS%d" % b2, "psS%d" % (b2 + 1)], writes=[ptn0, ptn1])
                pendp[n] = (ptp, ptn0, ptn1)
            m = n - LOOKP
            if m >= 0:
                ph, qg, kt = pairs[m]
                if qg == 0 and kt == 0 and ph >= 1:
                    load_v(ph)
                    if ph + 1 < NPH:
                        load_dense(ph + 1)
                    elif NPH == 4:
                        load_ad()
                ptp, ptn0, ptn1 = pendp.pop(m)
                aset = (ph * 4 + qg) % 2

                def pv2(e, ptp=ptp, aset=aset, kt=kt):
                    for u in range(2):
                        ins = e.matmul(acc[aset][u][:], lhsT=VB[:, kt, u, :], rhs=ptp[:, u, :],
                                       start=(kt == 0), stop=(kt == 63))
                    return ins
                S.op("pe", pv2, reads=["sV", ptn0, ptn1], writes=["acc%d_0" % aset, "acc%d_1" % aset])
                if kt == 63:
                    chunk = 2 + ph
                    rc = rec[aset]
                    for u in range(2):
                        bank = acc[aset][u]
                        osl = slice(u * 64, (u + 1) * 64)
                        ssl = slice((1 - u) * 64, (2 - u) * 64)
                        S.op("dve", lambda e, rc=rc, bank=bank, osl=osl, ssl=ssl: e.reciprocal(out=rc[osl, :], in_=bank[ssl, :]),
                             reads=["acc%d_%d" % (aset, u)], writes=["rec%d_%d" % (aset, u)])
                        S.op("dve", lambda e, rc=rc, bank=bank, chunk=chunk, qg=qg, osl=osl: e.tensor_tensor(
                            out=mixT[osl, chunk, qg * 512:(qg + 1) * 512], in0=bank[osl, :], in1=rc[osl, :], op=ALU.mult),
                            reads=["acc%d_%d" % (aset, u), "rec%d_%d" % (aset, u)],
                            writes=["mixT_%d_%d_%d" % (chunk, qg, u)])

        load_wo()
        load_d()
        a0, a1 = AR[0], AR[1]
        kA = a0[:, 0:5632].rearrange("p (c t) -> p c t", c=2)
        vA = a0[:, 5632:11264].rearrange("p (t n) -> p t n", n=256)
        qA = a0[:, 11264:19456].rearrange("p (h t) -> p h t", h=4)
        tA = a0[:, 19456:23040].rearrange("p (j n) -> p j n", j=7)
        kD = a1[:, 8192:11008]
        vD = a1[:, 11008:13824].rearrange("p (t n) -> p t n", n=128)
        qD = a1[:, 13824:22016].rearrange("p (h t) -> p h t", h=4)
        tD = a1[:, 22016:23552].rearrange("p (j n) -> p j n", j=3)
        for kind in ("A", "D"):
            cur = -1
            for i in range(16 if DO_AD else 0):
                slot = A_SLOT[i] if kind == "A" else D_SLOT[i]
                if slot != cur:
                    cur = slot
                    if kind == "A":
                        S.dma("pool", lambda e, cur=cur: e.dma_start(
                            out=tA, in_=tabA[cur].rearrange("p (j n) -> p j n", j=7)), writes=["ad_tA", "s0K", "s0Q"], semkey="D_tA")
                    else:
                        S.dma("sp", lambda e, cur=cur: e.dma_start(
                            out=tD, in_=tabD[cur].rearrange("p (j n) -> p j n", j=3)), writes=["ad_tD", "s1K", "s1Q"], semkey="D_tD")
                qs = slice(i * 128, (i + 1) * 128)
                nj = 7 if kind == "A" else 3
                aset = i % 2
                ao, asum = acc[aset]
                for j in range(nj):
                    bk = sbank % 4
                    sbank += 1
                    pt, ptn = PT.next()
                    kt = i + j if kind == "A" else i + 2 + j

                    def smm(e, kind=kind, bk=bk, j=j, kt=kt, qs=qs):
                        tt = tA if kind == "A" else tD
                        e.matmul(psS[:, bk, :], lhsT=ident[:], rhs=tt[:, j, :], start=True, stop=False)
                        for h in range(4):
                            if kind == "A":
                                lh = kA[:, h // 2, kt * 128:(kt + 1) * 128]
                                rh = qA[:, h, qs]
                            else:
                                lh = kD[:, kt * 128:(kt + 1) * 128]
                                rh = qD[:, h, qs]
                            ins = e.matmul(psS[:, bk, h * 128:(h + 1) * 128], lhsT=lh, rhs=rh, start=False, stop=(h == 3))
                        return ins
                    S.op("pe", smm, reads=(["ad_kA", "ad_qA", "ad_tA", "ident"] if kind == "A" else
                                           ["ad_kD", "ad_qD", "ad_tD", "ident"]), writes=["psS%d" % bk])
                    S.op("act", lambda e, bk=bk, pt=pt: e.activation(out=pt[:], in_=psS[:, bk, :], func=AF.Exp),
                         reads=["psS%d" % bk], writes=[ptn])

                    def pv(e, kind=kind, pt=pt, ao=ao, asum=asum, kt=kt, j=j, nj=nj):
                        for h in range(4):
                            r0, ch = (h % 2) * 64, h // 2
                            if kind == "A":
                                vv = vA[:, kt, h * 64:(h + 1) * 64]
                            else:
                                vv = vD[:, kt, ch * 64:(ch + 1) * 64]
                            st_ = (j == 0 and ch == 0)
                            e.matmul(ao[r0:r0 + 64, ch * 128:(ch + 1) * 128], lhsT=vv, rhs=pt[:, h * 128:(h + 1) * 128],
                                     start=st_, stop=(j == nj - 1), skip_group_check=True)
                            ins = e.matmul(asum[r0:r0 + 64, ch * 128:(ch + 1) * 128], lhsT=ones[:],
                                           rhs=pt[:, h * 128:(h + 1) * 128], start=st_, stop=(j == nj - 1),
                                           skip_group_check=True)
                        return ins
                    S.op("pe", pv, reads=["ad_vA" if kind == "A" else "ad_vD", ptn, "ones"],
                         writes=["acc%d_0" % aset, "acc%d_1" % aset])
                rc = rec[aset]
                ch0 = 0 if kind == "A" else 6
                if kind == "D":
                    for c in range(2):
                        S.op("dve", lambda e, rc=rc, asum=asum, c=c: e.tensor_scalar(
                            out=rc[:, c * 128:(c + 1) * 128], in0=asum[:, c * 128:(c + 1) * 128],
                            scalar1=sinke[:, c:c + 1], scalar2=None, op0=ALU.add),
                            reads=["acc%d_0" % aset, "acc%d_1" % aset, "sinke"], writes=["rec%d_0" % aset, "rec%d_1" % aset])
                    S.op("dve", lambda e, rc=rc: e.reciprocal(out=rc[:, 0:256], in_=rc[:, 0:256]),
                         reads=["rec%d_0" % aset, "rec%d_1" % aset], writes=["rec%d_0" % aset, "rec%d_1" % aset])
                else:
                    S.op("dve", lambda e, rc=rc, asum=asum: e.reciprocal(out=rc[:, 0:256], in_=asum[:, 0:256]),
                         reads=["acc%d_0" % aset, "acc%d_1" % aset], writes=["rec%d_0" % aset, "rec%d_1" % aset])
                S.op("dve", lambda e, rc=rc, ao=ao, ch0=ch0, qs=qs: e.tensor_tensor(
                    out=mixT[:, ch0:ch0 + 2, qs], in0=ao[:, 0:256].rearrange("p (c q) -> p c q", c=2),
                    in1=rc[:, 0:256].rearrange("p (c q) -> p c q", c=2), op=ALU.mult),
                    reads=["acc%d_0" % aset, "acc%d_1" % aset, "rec%d_0" % aset, "rec%d_1" % aset], writes=["mixT_%s_%d" % (kind, i)])

        mix_all = (["mixT_%d_%d_%d" % (c, q, u) for c in range(2, 6) for q in range(4) for u in range(2)] +
                   ["mixT_%s_%d" % (k, i) for k in "AD" for i in range(16)])
        S.dma("sp", lambda e: e.dma_start(out=o_mix.rearrange("p (c t) -> p c t", c=8), in_=mixT[:]),
              reads=mix_all, writes=["o_mix"])

        WO = AR[1][:, 0:8192].rearrange("p (c n) -> p c n", c=8)
        for i in range(16 if DO_WO else 0):
            b2 = i % 2
            hb = (i % 2) * 2
            S.dma("sp", lambda e, i=i, b2=b2: e.dma_start(out=xt[b2][:], in_=xin[i * 128:(i + 1) * 128, :]),
                  writes=["xt%d" % b2])

            def womm(e, i=i, hb=hb):
                for half in range(2):
                    for c in range(8):
                        ins = e.matmul(psS[:, hb + half, :], lhsT=mixT[:, c, i * 128:(i + 1) * 128],
                                       rhs=WO[:, c, half * 512:(half + 1) * 512], start=(c == 0), stop=(c == 7))
                return ins
            S.op("pe", womm, reads=mix_all + ["WO"], writes=["psS%d" % hb, "psS%d" % (hb + 1)])
            S.op("pool", lambda e, b2=b2: e.memset(ssq[b2][:], 0.0), writes=["ssq%d" % b2])
            for half in range(2):
                S.op("act", lambda e, b2=b2, hb=hb, half=half: e.activation(
                    out=junk[:], in_=psS[:, hb + half, :], func=AF.Square, accum_out=ssq[b2][:, half:half + 1]),
                    reads=["psS%d" % (hb + half), "ssq%d" % b2], writes=["junk", "ssq%d" % b2])
            S.op("dve", lambda e, b2=b2: e.tensor_tensor(out=rstd[b2][:], in0=ssq[b2][:, 0:1], in1=ssq[b2][:, 1:2],
                                                         op=ALU.add), reads=["ssq%d" % b2], writes=["rstd%d" % b2])
            S.op("act", lambda e, b2=b2: e.activation(out=rstd[b2][:], in_=rstd[b2][:], func=AF.Sqrt, bias=eps1[:],
                                                      scale=1.0 / DM), reads=["rstd%d" % b2, "eps1"], writes=["rstd%d" % b2])
            S.op("dve", lambda e, b2=b2: e.reciprocal(out=rstd[b2][:], in_=rstd[b2][:]),
                 reads=["rstd%d" % b2], writes=["rstd%d" % b2])
            for half in range(2):
                S.op("dve", lambda e, b2=b2, hb=hb, half=half: e.scalar_tensor_tensor(
                    out=yt[b2][:, half * 512:(half + 1) * 512], in0=psS[:, hb + half, :], scalar=rstd[b2][:, 0:1],
                    in1=gpo[:, half * 512:(half + 1) * 512], op0=ALU.mult, op1=ALU.mult),
                    reads=["psS%d" % (hb + half), "rstd%d" % b2, "gpo"], writes=["yt%d" % b2])
            S.op("pool", lambda e, b2=b2: e.tensor_tensor(out=yt[b2][:], in0=yt[b2][:], in1=xt[b2][:], op=ALU.add),
                 reads=["yt%d" % b2, "xt%d" % b2], writes=["yt%d" % b2])
            S.dma("sp", lambda e, i=i, b2=b2: e.dma_start(out=o_xm[i * 128:(i + 1) * 128, :], in_=yt[b2][:]),
                  reads=["yt%d" % b2], writes=["o_xm%d" % i], semkey="D_oxm%d" % b2)
        S.emit()
    return nc


def _a_table(rpb, gi):
    k = np.arange(128)
    q = np.arange(128)
    out = np.full((128, 7, 4, 128), NEG, np.float32)
    qq = gi * 128 + q
    qr, qc = qq // 64, qq % 64
    rs = np.clip(qr - 4, 0, 120)
    cs = np.clip(qc - 8, 0, 48)
    for j in range(7):
        kt = gi + j - 3
        if kt < 0 or kt >= 64:
            continue
        kk = kt * 128 + k
        kr_, kc = kk // 64, kk % 64
        valid = ((kr_[:, None] >= rs[None, :]) & (kr_[:, None] < rs[None, :] + 8) &
                 (kc[:, None] >= cs[None, :]) & (kc[:, None] < cs[None, :] + 16))
        ro = np.clip(kr_[:, None] - qr[None, :] + 7, 0, 14)
        co = np.clip(kc[:, None] - qc[None, :] + 15, 0, 30)
        for h in range(4):
            out[:, j, h, :] = np.where(valid, rpb[h][ro, co], np.float32(NEG))
    return out.reshape(128, 7 * 512)


def _d_table(gi):
    k = np.arange(128)
    q = np.arange(128)
    out = np.full((128, 3, 4, 128), NEG, np.float32)
    qpos = gi * 128 + q
    for j in range(3):
        kt = gi + j - 1
        if kt < 0 or kt >= 64:
            continue
        kpos = kt * 128 + k
        valid = np.abs(kpos[:, None] - qpos[None, :]) <= 128
        out[:, j, :, :] = np.where(valid, np.float32(0.0), np.float32(NEG))[:, None, :]
    return out.reshape(128, 3 * 512).astype(NPBF)


def _window(a, axis, start, length, total):
    lo, hi = max(start, 0), min(start + length, total)
    idx = [slice(None)] * a.ndim
    idx[axis] = slice(lo, hi)
    core = a[tuple(idx)]
    pads = [(0, 0)] * a.ndim
    pads[axis] = (lo - start, start + length - hi)
    return np.pad(core, pads)


def run_T(xfull, pres, l, inp):
    t0 = time.time()
    nc = _get("T", build_T)
    print("[kernel] build_T %.1fs" % (time.time() - t0), flush=True)
    rpb = inp["na_rpb"][l]
    wout = np.ascontiguousarray(inp["w_out"][l])
    gpost = np.ascontiguousarray(inp["mix_post_gain"][l][None, :])
    sink = inp["sw_sink"][l]
    p = np.arange(128)
    sinkc = np.stack([sink[2 * c + (p >= 64)] for c in range(2)], axis=1).astype(np.float32)
    in_maps = []
    for b in range(2):
        cores = [pres[b * 4 + i] for i in range(4)]
        cat = lambda k, ax: np.concatenate([np.asarray(cr[k]) for cr in cores], axis=ax)
        kB, kC, kA, kD = cat("kB", 1), cat("kC", 1), cat("kA", 1), cat("kD", 1)
        V, vB = cat("V", 0), cat("vB", 0)
        KDa = np.zeros((4, 128, 2 * SEQ), NPBF)
        VDa = np.ones((4, 128, 64, 2, 128), NPBF)
        for ph in range(4):
            if ph < 2:
                for u in range(2):
                    KDa[ph, 0:96, u * SEQ:(u + 1) * SEQ] = kB[(2 * ph + u) * 96:(2 * ph + u + 1) * 96]
                vs = vB[:, 2 * ph * 64:(2 * ph + 2) * 64]
            else:
                g = ph - 2
                for u in range(2):
                    KDa[ph, 0:64, u * SEQ:(u + 1) * SEQ] = kC[g * 64:(g + 1) * 64]
                vg = V[:, 256 + g * 64:256 + (g + 1) * 64]
                vs = np.concatenate([vg, vg], axis=1)
            vt = vs.reshape(64, 128, 128).transpose(1, 0, 2)
            VDa[ph, :, :, 0, 0:64] = vt[:, :, 0:64]
            VDa[ph, :, :, 1, 64:128] = vt[:, :, 64:128]
        for qd in range(4):
            c = b * 4 + qd
            r0 = qd * TCORE
            own = pres[c]
            QDa = np.zeros((4, 128, 2 * TCORE), NPBF)
            qB, qC = np.asarray(own["qB"]), np.asarray(own["qC"])
            for ph in range(4):
                for u in range(2):
                    if ph < 2:
                        QDa[ph, 0:96, u * TCORE:(u + 1) * TCORE] = qB[(2 * ph + u) * 96:(2 * ph + u + 1) * 96]
                    else:
                        h = 2 * (ph - 2) + u
                        QDa[ph, 0:64, u * TCORE:(u + 1) * TCORE] = qC[h * 64:(h + 1) * 64]
            w0, wl = r0 - 384, NWIN * 128
            kAw = _window(kA, 1, w0, wl, SEQ).reshape(2, 128, wl).transpose(1, 0, 2).reshape(128, 2 * wl)
            kDw = _window(kD, 1, w0, wl, SEQ)
            Vw = _window(V, 0, w0, wl, SEQ)
            vAw = Vw[:, 0:256].reshape(NWIN, 128, 256).transpose(1, 0, 2).reshape(128, NWIN * 256)
            vDw = Vw[:, 384:512].reshape(NWIN, 128, 128).transpose(1, 0, 2).reshape(128, NWIN * 128)
            qA_ = np.asarray(own["qA"])
            qD_ = np.asarray(own["qD"])
            qAd = np.zeros((128, 4, TCORE), NPBF)
            qDd = np.zeros((128, 4, TCORE), NPBF)
            for h in range(4):
                ra, rd = (h % 2) * 64, (h // 2) * 64
                qAd[ra:ra + 64, h] = qA_[h * 64:(h + 1) * 64]
                qDd[rd:rd + 64, h] = qD_[h * 64:(h + 1) * 64]
            qAd = qAd.reshape(128, 4 * TCORE)
            qDd = qDd.reshape(128, 4 * TCORE)
            g0 = qd * 16
            tabA = np.stack([_a_table(rpb, g0 + t) for t in (0, 1, 5, 14, 15)])
            tabD = _get("tabD%d" % qd, lambda g0=g0: np.stack([_d_table(g0 + t) for t in (0, 5, 15)]))
            in_maps.append(dict(KD=KDa, VD=VDa.reshape(4, 128, 64 * 256), QD=QDa, kAw=np.ascontiguousarray(kAw), vAw=np.ascontiguousarray(vAw),
                                kDw=np.ascontiguousarray(kDw), vDw=np.ascontiguousarray(vDw),
                                qAd=np.ascontiguousarray(qAd), qDd=np.ascontiguousarray(qDd), tabA=tabA, tabD=tabD,
                                sinkc=sinkc, x=np.ascontiguousarray(xfull[b, r0:r0 + TCORE]), wout=wout, gpost=gpost))
    t0 = time.time()
    res = run_bass_kernel_spmd(nc, in_maps, core_ids=list(range(NCORES)), **_TRACE_KW)
    print("[kernel] launch took %.1fs" % (time.time() - t0), flush=True)
    return res.results


NFC = DFF // 128
HW_ = TCORE + 2


def build_F():
    nc = bass.Bass("TRN2", target_bir_lowering=False)

    def din(name, shape, dt=F32):
        return nc.dram_tensor(name, shape, dt, kind="ExternalInput").ap()

    xm = din("xm", [TCORE, DM])
    xh = din("xh", [128, DM])
    g2 = din("g2", [1, DM])
    g3 = din("g3", [1, DM])
    wup = din("wup", [DM, NFC * 256])
    wdn = din("wdn", [DFF, DM])
    cpar = din("cpar", [128, NFC * 8])
    o_x = nc.dram_tensor("xo", [TCORE, DM], F32, kind="ExternalOutput").ap()

    with ExitStack() as st:
        def sb(n, s, d):
            return st.enter_context(nc.sbuf_tensor(n, s, d))

        def ps(n, s, d=F32):
            return st.enter_context(nc.psum_tensor(n, s, d))

        S = Sched(nc)
        h2T = sb("h2T", [128, 8, HW_], BF16)
        G = sb("G", [128, NFC, 1024], BF16)
        WD = sb("WD", [128, NFC, DM], BF16)
        WU = [sb("WU%d" % i, [128, 8, 256], BF16) for i in range(3)]
        ug = [sb("ug%d" % i, [128, 1026], F32) for i in range(2)]
        uv = [sb("uv%d" % i, [128, 1026], F32) for i in range(2)]
        cg = [sb("cg%d" % i, [128, 1024], F32) for i in range(2)]
        cv = [sb("cv%d" % i, [128, 1024], F32) for i in range(2)]
        xt = [sb("xt%d" % i, [128, DM], F32) for i in range(2)]
        yt = sb("yt", [128, DM], F32)
        hb = [sb("hb%d" % i, [128, DM], BF16) for i in range(2)]
        gp2 = sb("gp2", [128, DM], F32)
        gp3 = sb("gp3", [128, DM], F32)
        cp = sb("cp", [128, NFC, 2, 4], F32)
        eps1 = sb("eps1", [128, 1], F32)
        junk = sb("junk", [128, DM], BF16)
        ssq = [sb("ssq%d" % i, [128, 2], F32) for i in range(2)]
        rstd = [sb("rstd%d" % i, [128, 1], F32) for i in range(2)]
        pt = ps("pt", [128, DM], BF16)
        pu = Rot([(ps("pu%d" % i, [128, 512]), "pu%d" % i) for i in range(3)])
        py = [ps("py%d" % i, [128, 2, 512]) for i in range(2)]

        ident = _mk_consts(nc, S, sb)
        S.op("pool", lambda e: e.memset(eps1[:], EPS), writes=["eps1"])
        S.dma("sp", lambda e: e.dma_start(out=gp2[:], in_=g2.partition_broadcast(128)), writes=["gp2"])
        S.dma("sp", lambda e: e.dma_start(out=gp3[:], in_=g3.partition_broadcast(128)), writes=["gp3"])
        S.dma("sp", lambda e: e.dma_start(out=cp[:], in_=cpar.rearrange("p (f k w) -> p f k w", f=NFC, k=2)),
              writes=["cp"])

        h2names = []
        for i in range(17):
            b2 = i % 2
            src = xm[i * 128:(i + 1) * 128, :] if i < 16 else xh
            S.dma("sp", lambda e, src=src, b2=b2: e.dma_start(out=xt[b2][:], in_=src), writes=["xt%d" % b2])
            S.op("pool", lambda e, b2=b2: e.memset(ssq[b2][:], 0.0), writes=["ssq%d" % b2])
            S.op("act", lambda e, b2=b2: e.activation(out=junk[:], in_=xt[b2][:], func=AF.Square,
                                                      accum_out=ssq[b2][:, 0:1]),
                 reads=["xt%d" % b2, "ssq%d" % b2], writes=["junk", "ssq%d" % b2])
            S.op("act", lambda e, b2=b2: e.activation(out=rstd[b2][:], in_=ssq[b2][:, 0:1], func=AF.Sqrt,
                                                      bias=eps1[:], scale=1.0 / DM),
                 reads=["ssq%d" % b2, "eps1"], writes=["rstd%d" % b2])
            S.op("dve", lambda e, b2=b2: e.reciprocal(out=rstd[b2][:], in_=rstd[b2][:]),
                 reads=["rstd%d" % b2], writes=["rstd%d" % b2])
            S.op("dve", lambda e, b2=b2: e.scalar_tensor_tensor(
                out=hb[b2][:], in0=xt[b2][:], scalar=rstd[b2][:, 0:1], in1=gp2[:], op0=ALU.mult, op1=ALU.mult),
                reads=["xt%d" % b2, "rstd%d" % b2, "gp2"], writes=["hb%d" % b2])

            def tr(e, b2=b2):
                for c in range(8):
                    ins = e.transpose(out=pt[:, c * 128:(c + 1) * 128], in_=hb[b2][:, c * 128:(c + 1) * 128],
                                      identity=ident[:])
                return ins
            S.op("pe", tr, reads=["hb%d" % b2, "ident"], writes=["pt"])
            ptv = pt[:].rearrange("p (c t) -> p c t", c=8)
            nm = "h2T_%d" % i
            h2names.append(nm)
            if i < 16:
                S.op("act", lambda e, i=i, ptv=ptv: e.copy(out=h2T[:, :, 1 + i * 128:1 + (i + 1) * 128], in_=ptv),
                     reads=["pt"], writes=[nm])
            else:
                S.op("act", lambda e, ptv=ptv: e.copy(out=h2T[:, :, 0:HW_:HW_ - 1], in_=ptv[:, :, 0:2]),
                     reads=["pt"], writes=[nm])

        wdv = wdn.rearrange("(f p) n -> p f n", p=128)
        for f in range(NFC):
            S.dma("pool", lambda e, f=f: e.dma_start(out=WD[:, f, :], in_=wdv[:, f, :]), writes=["WD"], semkey="D_WD")

        wuv = wup.rearrange("(c p) n -> p c n", p=128)
        k = 0
        for hf in range(2):
            c0 = hf * 1024
            blocks = ((0, 512), (512, 512), (1024, 2))
            for fc in range(NFC):
                wb = k % 3
                ub = k % 2
                k += 1
                S.dma("pool", lambda e, wb=wb, fc=fc: e.dma_start(out=WU[wb][:], in_=wuv[:, :, fc * 256:(fc + 1) * 256]),
                      writes=["WU%d" % wb])
                for (kind, ubuf, unm, col) in (("g", ug, "ug", 0), ("v", uv, "uv", 128)):
                    for (b0, bn) in blocks:
                        p, pnm = pu.next()

                        def mm(e, p=p, wb=wb, col=col, b0=b0, bn=bn, c0=c0):
                            for c in range(8):
                                ins = e.matmul(p[:, 0:bn], lhsT=WU[wb][:, c, col:col + 128],
                                               rhs=h2T[:, c, c0 + b0:c0 + b0 + bn], start=(c == 0), stop=(c == 7))
                            return ins
                        S.op("pe", mm, reads=["WU%d" % wb] + h2names, writes=[pnm])
                        S.op("act", lambda e, p=p, ubuf=ubuf, ub=ub, b0=b0, bn=bn: e.copy(
                            out=ubuf[ub][:, b0:b0 + bn], in_=p[:, 0:bn]), reads=[pnm], writes=["%s%d" % (unm, ub)])
                for (ubuf, unm, cbuf, cnm, kk) in ((ug, "ug", cg, "cg", 0), (uv, "uv", cv, "cv", 1)):
                    u_, c_ = ubuf[ub], cbuf[ub]
                    if kk == 0:
                        S.op("dve", lambda e, u_=u_, c_=c_, fc=fc, kk=kk: e.tensor_scalar(
                            out=c_[:], in0=u_[:, 1:1025], scalar1=cp[:, fc, kk, 1:2], scalar2=cp[:, fc, kk, 3:4],
                            op0=ALU.mult, op1=ALU.add), reads=["%s%d" % (unm, ub), "cp"], writes=["%s%d" % (cnm, ub)])
                    else:
                        S.op("act", lambda e, u_=u_, c_=c_, fc=fc, kk=kk: e.activation(
                            out=c_[:], in_=u_[:, 1:1025], func=AF.Identity, scale=cp[:, fc, kk, 1:2],
                            bias=cp[:, fc, kk, 3:4]), reads=["%s%d" % (unm, ub), "cp"], writes=["%s%d" % (cnm, ub)])
                    S.op("dve", lambda e, u_=u_, c_=c_, fc=fc, kk=kk: e.scalar_tensor_tensor(
                        out=c_[:], in0=u_[:, 0:1024], scalar=cp[:, fc, kk, 0:1], in1=c_[:], op0=ALU.mult, op1=ALU.add),
                        reads=["%s%d" % (unm, ub), "%s%d" % (cnm, ub), "cp"], writes=["%s%d" % (cnm, ub)])
                    S.op("dve", lambda e, u_=u_, c_=c_, fc=fc, kk=kk: e.scalar_tensor_tensor(
                        out=c_[:], in0=u_[:, 2:1026], scalar=cp[:, fc, kk, 2:3], in1=c_[:], op0=ALU.mult, op1=ALU.add),
                        reads=["%s%d" % (unm, ub), "%s%d" % (cnm, ub), "cp"], writes=["%s%d" % (cnm, ub)])
                S.op("act", lambda e, ub=ub: e.activation(out=cg[ub][:], in_=cg[ub][:], func=AF.Gelu_apprx_tanh),
                     reads=["cg%d" % ub], writes=["cg%d" % ub])
                S.op("dve", lambda e, ub=ub, fc=fc: e.tensor_tensor(out=G[:, fc, :], in0=cg[ub][:], in1=cv[ub][:],
                                                                    op=ALU.mult),
                     reads=["cg%d" % ub, "cv%d" % ub], writes=["G_%d" % fc])
            gnames = ["G_%d" % f for f in range(NFC)]
            for il in range(8):
                i = hf * 8 + il
                b2 = i % 2
                pyb = py[b2]
                S.dma("sp", lambda e, i=i, b2=b2: e.dma_start(out=xt[b2][:], in_=xm[i * 128:(i + 1) * 128, :]),
                      writes=["xt%d" % b2])

                def dmm(e, il=il, pyb=pyb):
                    for half in range(2):
                        for f in range(NFC):
                            ins = e.matmul(pyb[:, half, :], lhsT=G[:, f, il * 128:(il + 1) * 128],
                                           rhs=WD[:, f, half * 512:(half + 1) * 512], start=(f == 0), stop=(f == NFC - 1))
                    return ins
                S.op("pe", dmm, reads=gnames + ["WD"], writes=["py%d" % b2])
                S.op("pool", lambda e, b2=b2: e.memset(ssq[b2][:], 0.0), writes=["ssq%d" % b2])
                for half in range(2):
                    S.op("act", lambda e, b2=b2, pyb=pyb, half=half: e.activation(
                        out=junk[:, 0:512], in_=pyb[:, half, :], func=AF.Square, accum_out=ssq[b2][:, half:half + 1]),
                        reads=["py%d" % b2, "ssq%d" % b2], writes=["junk", "ssq%d" % b2])
                S.op("dve", lambda e, b2=b2: e.tensor_tensor(out=rstd[b2][:], in0=ssq[b2][:, 0:1], in1=ssq[b2][:, 1:2],
                                                             op=ALU.add), reads=["ssq%d" % b2], writes=["rstd%d" % b2])
                S.op("act", lambda e, b2=b2: e.activation(out=rstd[b2][:], in_=rstd[b2][:], func=AF.Sqrt, bias=eps1[:],
                                                          scale=1.0 / DM), reads=["rstd%d" % b2, "eps1"], writes=["rstd%d" % b2])
                S.op("dve", lambda e, b2=b2: e.reciprocal(out=rstd[b2][:], in_=rstd[b2][:]),
                     reads=["rstd%d" % b2], writes=["rstd%d" % b2])
                for half in range(2):
                    S.op("dve", lambda e, b2=b2, pyb=pyb, half=half: e.scalar_tensor_tensor(
                        out=yt[:, half * 512:(half + 1) * 512], in0=pyb[:, half, :], scalar=rstd[b2][:, 0:1],
                        in1=gp3[:, half * 512:(half + 1) * 512], op0=ALU.mult, op1=ALU.mult),
                        reads=["py%d" % b2, "rstd%d" % b2, "gp3"], writes=["yt"])
                S.op("pool", lambda e, b2=b2: e.tensor_tensor(out=yt[:], in0=yt[:], in1=xt[b2][:], op=ALU.add),
                     reads=["yt", "xt%d" % b2], writes=["yt"])
                S.dma("sp", lambda e, i=i: e.dma_start(out=o_x[i * 128:(i + 1) * 128, :], in_=yt[:]),
                      reads=["yt"], writes=["o_x%d" % i], semkey="D_ox")
        S.emit()
    return nc


def run_F(xm_full, l, inp):
    nc = _get("F", build_F)
    wu = inp["w_up"][l]
    idx = np.concatenate([np.concatenate([np.arange(f * 128, (f + 1) * 128), DFF + np.arange(f * 128, (f + 1) * 128)])
                          for f in range(NFC)])
    wup = np.ascontiguousarray(wu[:, idx])
    wdn = np.ascontiguousarray(inp["w_down"][l])
    cw, cb = inp["conv_w"][l], inp["conv_b"][l]
    cpar = np.zeros((128, NFC, 2, 4), np.float32)
    for k in range(2):
        for w in range(3):
            cpar[:, :, k, w] = cw[w, k * DFF:(k + 1) * DFF].reshape(NFC, 128).T
        cpar[:, :, k, 3] = cb[k * DFF:(k + 1) * DFF].reshape(NFC, 128).T
    cpar = cpar.reshape(128, NFC * 8)
    g2 = np.ascontiguousarray(inp["ffn_pre_gain"][l][None, :])
    g3 = np.ascontiguousarray(inp["ffn_post_gain"][l][None, :])
    in_maps = []
    for c in range(NCORES):
        b, r0 = c // 4, (c % 4) * TCORE
        xh = np.zeros((128, DM), np.float32)
        if r0 > 0:
            xh[0] = xm_full[b, r0 - 1]
        if r0 + TCORE < SEQ:
            xh[1] = xm_full[b, r0 + TCORE]
        in_maps.append(dict(xm=np.ascontiguousarray(xm_full[b, r0:r0 + TCORE]), xh=xh, g2=g2, g3=g3, wup=wup,
                            wdn=wdn, cpar=cpar))
    t0 = time.time()
    res = run_bass_kernel_spmd(nc, in_maps, core_ids=list(range(NCORES)), **_TRACE_KW)
    print("[kernel] launch took %.1fs" % (time.time() - t0), flush=True)
    out = np.zeros((2, SEQ, DM), np.float32)
    for c in range(NCORES):
        b, r0 = c // 4, (c % 4) * TCORE
        out[b, r0:r0 + TCORE] = np.asarray(res.results[c]["xo"])
    return out


def kernel(**inputs):
    inp = {k: np.asarray(v) for k, v in inputs.items()}
    x = np.ascontiguousarray(inp["x"], dtype=np.float32)
    for l in range(2):
        pres = run_P(x, l, inp)
        tres = run_T(x, pres, l, inp)
        xm = np.zeros((2, SEQ, DM), np.float32)
        for c in range(NCORES):
            b, r0 = c // 4, (c % 4) * TCORE
            xm[b, r0:r0 + TCORE] = np.asarray(tres[c]["xm"])
        x = run_F(xm, l, inp)
    return x
```

```python
import os
import time
import numpy as np
import ml_dtypes
from contextlib import ExitStack
import concourse.bass as bass
import concourse.mybir as mybir
from concourse.bass_utils import run_bass_kernel_spmd

F32 = mybir.dt.float32
BF16 = mybir.dt.bfloat16
ALU = mybir.AluOpType
AF = mybir.ActivationFunctionType
NPBF = ml_dtypes.bfloat16

NCORES = 8
DM = 1024
SEQ = 8192
TCORE = 2048
NBLK = TCORE // 512
EPS = 1e-6
DFF = 2816


class Sched:
    ENGS = ("pe", "act", "dve", "pool", "sp")

    def __init__(self, nc):
        self.nc = nc
        self.ops = {e: [] for e in self.ENGS}
        self.cnt = {e: 0 for e in self.ENGS}
        self.lastw = {}
        self.readers = {}
        self.waited = {e: {} for e in self.ENGS}
        self.dma_cnt = {}
        self.semkeys = []

    def _deps(self, eng, reads, writes):
        toks = []
        for r in reads:
            if r in self.lastw:
                toks.append(self.lastw[r])
        for w in writes:
            if w in self.lastw:
                toks.append(self.lastw[w])
            toks.extend(self.readers.get(w, ()))
        need = {}
        for k, v in toks:
            if eng == "pe" and k == "E_pe":
                continue
            if v > need.get(k, 0):
                need[k] = v
        waits = []
        for k, v in need.items():
            if self.waited[eng].get(k, 0) < v:
                self.waited[eng][k] = v
                waits.append((k, v))
        return waits

    def _commit(self, tok, reads, writes):
        for r in reads:
            self.readers.setdefault(r, []).append(tok)
        for w in writes:
            self.lastw[w] = tok
            self.readers[w] = []

    def op(self, eng, fn, reads=(), writes=()):
        waits = self._deps(eng, reads, writes)
        self.cnt[eng] += 1
        key = "E_" + eng
        if key not in self.semkeys:
            self.semkeys.append(key)
        tok = (key, self.cnt[eng])
        self._commit(tok, reads, writes)
        self.ops[eng].append((waits, fn, key, 1))
        return tok

    def dma(self, eng, fn, reads=(), writes=(), semkey=None):
        waits = self._deps(eng, reads, writes)
        if semkey is None:
            semkey = "D_" + str(writes[0] if writes else reads[0])
        if semkey not in self.semkeys:
            self.semkeys.append(semkey)
        self.dma_cnt[semkey] = self.dma_cnt.get(semkey, 0) + 16
        tok = (semkey, self.dma_cnt[semkey])
        self._commit(tok, reads, writes)
        self.ops[eng].append((waits, fn, semkey, 16))
        return tok

    def coll(self, eng, fn, reads=(), writes=(), semkey=None):
        waits = self._deps(eng, reads, writes)
        if semkey is None:
            semkey = "C_" + str(writes[0])
        if semkey not in self.semkeys:
            self.semkeys.append(semkey)
        self.dma_cnt[semkey] = self.dma_cnt.get(semkey, 0) + 1
        tok = (semkey, self.dma_cnt[semkey])
        self._commit(tok, reads, writes)
        self.ops[eng].append((waits, fn, semkey, 1))
        return tok

    def emit(self, final_eng="sp"):
        nc = self.nc
        with ExitStack() as st:
            sems = {}
            for i, k in enumerate(self.semkeys):
                sems[k] = st.enter_context(nc.semaphore("s%d" % i))
            final_waits = [(k, v) for k, v in self.dma_cnt.items()]
            final_waits += [("E_" + e, self.cnt[e]) for e in self.ENGS if self.cnt[e]]
            block = st.enter_context(nc.Block())
            reg = {"pe": block.tensor, "act": block.scalar, "dve": block.vector,
                   "pool": block.gpsimd, "sp": block.sync}
            for e in self.ENGS:
                oplist = self.ops[e]
                fw = final_waits if e == final_eng else []
                if not oplist and not fw:
                    continue

                def body(eng, oplist=oplist, fw=fw):
                    for waits, fn, key, inc in oplist:
                        for k, v in waits:
                            eng.wait_ge(sems[k], v)
                        ins = fn(eng)
                        ins.then_inc(sems[key], inc)
                    for k, v in fw:
                        eng.wait_ge(sems[k], v)
                reg[e](body)
        return len(self.semkeys)


class Rot:
    def __init__(self, items):
        self.items = items
        self.i = 0

    def next(self):
        it = self.items[self.i % len(self.items)]
        self.i += 1
        return it


def _mk_consts(nc, S, sb):
    identf = sb("identf", [128, 128], F32)
    ident = sb("ident", [128, 128], BF16)
    S.op("pool", lambda e: e.memset(identf[:], 0.0), writes=["identf"])
    S.op("pool", lambda e: e.affine_select(out=identf[:], in_=identf[:], pattern=[[-1, 128]],
                                           compare_op=ALU.not_equal, fill=1.0, base=0,
                                           channel_multiplier=1),
         reads=["identf"], writes=["identf"])
    S.op("dve", lambda e: e.tensor_copy(out=ident[:], in_=identf[:]), reads=["identf"], writes=["ident"])
    return ident


cAq, cAk, cCq, cCqp, cCk, cCkp, cDq, cDqp, cDk, cDkp = 0, 256, 512, 768, 1024, 1152, 1280, 1536, 1792, 1920
cBcq, cBckv, cBkr, cBkrp, cV, NCX = 2048, 2304, 2432, 2528, 2624, 3136
SC64 = 0.125
SC96 = 96 ** -0.5


def build_P():
    nc = bass.Bass("TRN2", target_bir_lowering=False)

    def din(name, shape, dt=F32):
        return nc.dram_tensor(name, shape, dt, kind="ExternalInput").ap()

    def dout(name, shape, dt=BF16):
        return nc.dram_tensor(name, shape, dt, kind="ExternalOutput").ap()

    x = din("x", [TCORE, DM])
    gpre = din("gpre", [1, DM])
    wext = din("wext", [DM, NCX])
    wuq = din("wuq", [256, 768])
    wukv = din("wukv", [128, 512])
    tab = din("tab", [6, 128, TCORE])
    pcol = din("pcol", [128, 8])
    o_qA = dout("qA", [256, TCORE]); o_kA = dout("kA", [256, TCORE])
    o_qC = dout("qC", [256, TCORE]); o_kC = dout("kC", [128, TCORE])
    o_qD = dout("qD", [256, TCORE]); o_kD = dout("kD", [128, TCORE])
    o_qB = dout("qB", [384, TCORE]); o_kB = dout("kB", [384, TCORE])
    o_V = dout("V", [TCORE, 512]); o_vB = dout("vB", [TCORE, 256])

    with ExitStack() as st:
        def sb(n, s, d):
            return st.enter_context(nc.sbuf_tensor(n, s, d))

        def ps(n, s, d=F32):
            return st.enter_context(nc.psum_tensor(n, s, d))

        S = Sched(nc)
        W = sb("W", [128, 8, NCX], BF16)
        WQ = sb("WQ", [128, 2, 768], BF16)
        WKV = sb("WKV", [128, 512], BF16)
        gp = sb("gp", [128, DM], F32)
        pc = sb("pc", [128, 8], F32)
        ones = sb("ones", [128, 128], BF16)
        onesblk = sb("onesblk", [128, 128], BF16)
        eps1 = sb("eps1", [128, 1], F32)
        eps64 = sb("eps64", [128, 1], F32)
        xt = [sb("xt%d" % i, [128, DM], F32) for i in range(2)]
        xsq = sb("xsq", [128, DM], BF16)
        ssq = [sb("ssq%d" % i, [128, 1], F32) for i in range(2)]
        rstd = [sb("rstd%d" % i, [128, 1], F32) for i in range(2)]
        hb = [sb("hb%d" % i, [128, DM], BF16) for i in range(2)]
        hT = [sb("hT%d" % i, [128, 8, 512], BF16) for i in range(2)]
        tabs = sb("tabs", [128, 6, 512], F32)
        qA_s = [sb("qA_s%d" % i, [128, 2, 512], BF16) for i in range(2)]
        kA_s = [sb("kA_s%d" % i, [128, 2, 512], BF16) for i in range(2)]
        qC_s = [sb("qC_s%d" % i, [128, 2, 512], BF16) for i in range(2)]
        kC_s = [sb("kC_s%d" % i, [128, 1, 512], BF16) for i in range(2)]
        qD_s = [sb("qD_s%d" % i, [128, 2, 512], BF16) for i in range(2)]
        kD_s = [sb("kD_s%d" % i, [128, 1, 512], BF16) for i in range(2)]
        qB_s = [sb("qB_s%d" % i, [96, 4, 512], BF16) for i in range(2)]
        kB_s = [sb("kB_s%d" % i, [96, 4, 512], BF16) for i in range(2)]
        V_s = [sb("V_s%d" % i, [128, 4, 512], BF16) for i in range(2)]
        vB_s = [sb("vB_s%d" % i, [128, 4, 256], BF16) for i in range(2)]
        sqb = Rot([(sb("sqb%d" % i, [128, 512], BF16), "sqb%d" % i) for i in range(3)])
        rsb = Rot([(sb("rsb%d" % i, [128, 512], F32), "rsb%d" % i) for i in range(2)])
        t1b = Rot([(sb("t1b%d" % i, [128, 512], F32), "t1b%d" % i) for i in range(3)])
        t2b = Rot([(sb("t2b%d" % i, [128, 512], F32), "t2b%d" % i) for i in range(3)])
        cqn = sb("cqn", [128, 2, 512], BF16)
        ckvn = sb("ckvn", [128, 512], BF16)
        pt = ps("pt", [128, DM], BF16)
        pj = Rot([(ps("pj%d" % i, [128, 512]), "pj%d" % i) for i in range(5)])
        pn = ps("pn", [128, 512])
        pv = ps("pv", [128, 512])

        ident = _mk_consts(nc, S, sb)
        S.op("pool", lambda e: e.memset(ones[:], 1.0), writes=["ones"])
        S.op("pool", lambda e: e.memset(onesblk[:], 0.0), writes=["onesblk"])
        S.op("pool", lambda e: e.memset(onesblk[0:64, 0:64], 1.0), reads=["onesblk"], writes=["onesblk"])
        S.op("pool", lambda e: e.memset(onesblk[64:128, 64:128], 1.0), reads=["onesblk"], writes=["onesblk"])
        S.op("pool", lambda e: e.memset(eps1[:], EPS), writes=["eps1"])
        S.op("pool", lambda e: e.memset(eps64[:], 64 * EPS), writes=["eps64"])
        S.dma("sp", lambda e: e.dma_start(out=gp[:], in_=gpre.partition_broadcast(128)), writes=["gp"])
        S.dma("sp", lambda e: e.dma_start(out=pc[:], in_=pcol), writes=["pc"])
        wv = wext.rearrange("(c p) n -> p c n", p=128)
        WG = ((0, 512), (1280, 2048), (512, 1280), (2048, 2624), (2624, NCX))
        for gi, (c0, c1) in enumerate(WG):
            S.dma("pool", lambda e, c0=c0, c1=c1: e.dma_start(out=W[:, :, c0:c1], in_=wv[:, :, c0:c1]),
                  writes=["W%d" % gi], semkey="D_W%d" % gi)

        def wres(col):
            for gi, (c0, c1) in enumerate(WG):
                if c0 <= col < c1:
                    return "W%d" % gi
        S.dma("pool", lambda e: e.dma_start(out=WQ[:], in_=wuq.rearrange("(c p) n -> p c n", p=128)),
              writes=["WQ"])
        S.dma("pool", lambda e: e.dma_start(out=WKV[:], in_=wukv), writes=["WKV"])

        def proj(pap, col0, M, par):
            def fn(e):
                for c in range(8):
                    ins = e.matmul(pap[0:M, :], lhsT=W[:, c, col0:col0 + M], rhs=hT[par][:, c, :],
                                   start=(c == 0), stop=(c == 7))
                return ins
            return fn

        def step1(tb):
            par = tb % 2
            hTr = ["hT%d_%d" % (par, j) for j in range(4)]
            for j in range(4):
                i = tb * 4 + j
                b2 = i % 2
                S.dma("sp", lambda e, i=i, b2=b2: e.dma_start(out=xt[b2][:], in_=x[i * 128:(i + 1) * 128, :]),
                      writes=["xt%d" % b2])
                S.op("pool", lambda e, b2=b2: e.memset(ssq[b2][:], 0.0), writes=["ssq%d" % b2])
                S.op("act", lambda e, b2=b2: e.activation(out=xsq[:], in_=xt[b2][:], func=AF.Square,
                                                          accum_out=ssq[b2][:]),
                     reads=["xt%d" % b2, "ssq%d" % b2], writes=["xsq", "ssq%d" % b2])
                S.op("act", lambda e, b2=b2: e.activation(out=rstd[b2][:], in_=ssq[b2][:], func=AF.Sqrt,
                                                          bias=eps1[:], scale=1.0 / DM),
                     reads=["ssq%d" % b2, "eps1"], writes=["rstd%d" % b2])
                S.op("dve", lambda e, b2=b2: e.reciprocal(out=rstd[b2][:], in_=rstd[b2][:]),
                     reads=["rstd%d" % b2], writes=["rstd%d" % b2])
                S.op("dve", lambda e, b2=b2: e.scalar_tensor_tensor(
                    out=hb[b2][:], in0=xt[b2][:], scalar=rstd[b2][:, 0:1], in1=gp[:], op0=ALU.mult, op1=ALU.mult),
                    reads=["xt%d" % b2, "rstd%d" % b2, "gp"], writes=["hb%d" % b2])

                def tr(e, b2=b2):
                    for c in range(8):
                        ins = e.transpose(out=pt[:, c * 128:(c + 1) * 128], in_=hb[b2][:, c * 128:(c + 1) * 128],
                                          identity=ident[:])
                    return ins
                S.op("pe", tr, reads=["hb%d" % b2, "ident"], writes=["pt"])
                S.op("act", lambda e, par=par, j=j: e.copy(out=hT[par][:, :, j * 128:(j + 1) * 128],
                                                           in_=pt[:].rearrange("p (c t) -> p c t", c=8)),
                     reads=["pt"], writes=[hTr[j]])

        def jobs(tb):
            par = tb % 2
            hTr = ["hT%d_%d" % (par, j) for j in range(4)]
            S.dma("sp", lambda e, tb=tb: e.dma_start(
                out=tabs[:], in_=tab.rearrange("k p t -> p k t")[:, :, tb * 512:(tb + 1) * 512]),
                writes=["tabs"])

            def rWc(col):
                return [wres(col)] + hTr

            for (col, dst, dname, sc) in ((cAq, qA_s, "qA_s", SC64), (cAk, kA_s, "kA_s", 1.0)):
                for rb in range(2):
                    p, pnm = pj.next()
                    S.op("pe", proj(p, col + rb * 128, 128, par), reads=rWc(col), writes=[pnm])
                    S.op("act", lambda e, p=p, dst=dst, rb=rb, sc=sc, par=par: e.mul(
                        out=dst[par][:, rb, :], in_=p[:], mul=sc), reads=[pnm], writes=["%s%d" % (dname, par)])
            for (col, colp, nrb, dst, dname, sc) in ((cDq, cDqp, 2, qD_s, "qD_s", SC64), (cDk, cDkp, 1, kD_s, "kD_s", 1.0)):
                for rb in range(nrb):
                    pm, pmn = pj.next()
                    pp, ppn = pj.next()
                    t1, t1n = t1b.next()
                    t2, t2n = t2b.next()
                    S.op("pe", proj(pm, col + rb * 128, 128, par), reads=rWc(col), writes=[pmn])
                    S.op("pe", proj(pp, colp + rb * 128, 128, par), reads=rWc(colp), writes=[ppn])
                    S.op("dve", lambda e, pm=pm, t1=t1, sc=sc: e.scalar_tensor_tensor(
                        out=t1[:], in0=pm[:], scalar=sc, in1=tabs[:, 0, :], op0=ALU.mult, op1=ALU.mult),
                        reads=[pmn, "tabs"], writes=[t1n])
                    S.op("dve", lambda e, pp=pp, t2=t2, sc=sc: e.scalar_tensor_tensor(
                        out=t2[:], in0=pp[:], scalar=sc, in1=tabs[:, 1, :], op0=ALU.mult, op1=ALU.mult),
                        reads=[ppn, "tabs"], writes=[t2n])
                    S.op("pool", lambda e, t1=t1, t2=t2, dst=dst, rb=rb, par=par: e.tensor_tensor(
                        out=dst[par][:, rb, :], in0=t1[:], in1=t2[:], op=ALU.add),
                        reads=[t1n, t2n], writes=["%s%d" % (dname, par)])
            for (col, colp, nrb, dst, dname, g0, epsb, epsn, nsc) in (
                    (cCq, cCqp, 2, qC_s, "qC_s", 0, eps64, "eps64", 1.0),
                    (cCk, cCkp, 1, kC_s, "kC_s", 2, eps1, "eps1", 1.0 / 64)):
                for rb in range(nrb):
                    pm, pmn = pj.next()
                    pp, ppn = pj.next()
                    t1, t1n = t1b.next()
                    t2, t2n = t2b.next()
                    sq, sqn = sqb.next()
                    rs, rsn = rsb.next()
                    S.op("pe", proj(pm, col + rb * 128, 128, par), reads=rWc(col), writes=[pmn])
                    S.op("pe", proj(pp, colp + rb * 128, 128, par), reads=rWc(colp), writes=[ppn])
                    S.op("act", lambda e, pm=pm, sq=sq: e.activation(out=sq[:], in_=pm[:], func=AF.Square),
                         reads=[pmn], writes=[sqn])
                    S.op("pe", lambda e, sq=sq: e.matmul(pn[:], lhsT=onesblk[:], rhs=sq[:], start=True, stop=True),
                         reads=[sqn, "onesblk"], writes=["pn"])
                    S.op("act", lambda e, rs=rs, epsb=epsb, nsc=nsc: e.activation(
                        out=rs[:], in_=pn[:], func=AF.Sqrt, bias=epsb[:], scale=nsc),
                        reads=["pn", epsn], writes=[rsn])
                    S.op("dve", lambda e, rs=rs: e.reciprocal(out=rs[:], in_=rs[:]), reads=[rsn], writes=[rsn])
                    S.op("dve", lambda e, pm=pm, t1=t1, g0=g0: e.scalar_tensor_tensor(
                        out=t1[:], in0=pm[:], scalar=pc[:, g0:g0 + 1], in1=tabs[:, 2, :], op0=ALU.mult, op1=ALU.mult),
                        reads=[pmn, "tabs", "pc"], writes=[t1n])
                    S.op("dve", lambda e, pp=pp, t2=t2, g0=g0: e.scalar_tensor_tensor(
                        out=t2[:], in0=pp[:], scalar=pc[:, g0 + 1:g0 + 2], in1=tabs[:, 3, :], op0=ALU.mult, op1=ALU.mult),
                        reads=[ppn, "tabs", "pc"], writes=[t2n])
                    S.op("pool", lambda e, t1=t1, t2=t2: e.tensor_tensor(out=t1[:], in0=t1[:], in1=t2[:], op=ALU.add),
                         reads=[t1n, t2n], writes=[t1n])
                    S.op("pool", lambda e, t1=t1, rs=rs, dst=dst, rb=rb, par=par: e.tensor_tensor(
                        out=dst[par][:, rb, :], in0=t1[:], in1=rs[:], op=ALU.mult),
                        reads=[t1n, rsn], writes=["%s%d" % (dname, par)])
            pm0, pm0n = pj.next()
            pm1, pm1n = pj.next()
            sq0, sq0n = sqb.next()
            sq1, sq1n = sqb.next()
            rs, rsn = rsb.next()
            S.op("pe", proj(pm0, cBcq, 128, par), reads=rWc(cBcq), writes=[pm0n])
            S.op("pe", proj(pm1, cBcq + 128, 128, par), reads=rWc(cBcq), writes=[pm1n])
            S.op("act", lambda e, pm0=pm0, sq0=sq0: e.activation(out=sq0[:], in_=pm0[:], func=AF.Square),
                 reads=[pm0n], writes=[sq0n])
            S.op("act", lambda e, pm1=pm1, sq1=sq1: e.activation(out=sq1[:], in_=pm1[:], func=AF.Square),
                 reads=[pm1n], writes=[sq1n])

            def nrm2(e, sq0=sq0, sq1=sq1):
                e.matmul(pn[:], lhsT=ones[:], rhs=sq0[:], start=True, stop=False)
                return e.matmul(pn[:], lhsT=ones[:], rhs=sq1[:], start=False, stop=True)
            S.op("pe", nrm2, reads=[sq0n, sq1n, "ones"], writes=["pn"])
            S.op("act", lambda e, rs=rs: e.activation(out=rs[:], in_=pn[:], func=AF.Sqrt, bias=eps1[:], scale=1.0 / 256),
                 reads=["pn", "eps1"], writes=[rsn])
            S.op("dve", lambda e, rs=rs: e.reciprocal(out=rs[:], in_=rs[:]), reads=[rsn], writes=[rsn])
            for c, (pm, pmn) in enumerate(((pm0, pm0n), (pm1, pm1n))):
                S.op("dve", lambda e, pm=pm, c=c, rs=rs: e.scalar_tensor_tensor(
                    out=cqn[:, c, :], in0=pm[:], scalar=pc[:, 4 + c:5 + c], in1=rs[:], op0=ALU.mult, op1=ALU.mult),
                    reads=[pmn, rsn, "pc"], writes=["cqn"])
            for h in range(4):
                pm, pmn = pj.next()
                pp, ppn = pj.next()
                t1, t1n = t1b.next()
                t2, t2n = t2b.next()

                def up(pap, c0):
                    def fn(e):
                        e.matmul(pap[0:96, :], lhsT=WQ[:, 0, c0:c0 + 96], rhs=cqn[:, 0, :], start=True, stop=False)
                        return e.matmul(pap[0:96, :], lhsT=WQ[:, 1, c0:c0 + 96], rhs=cqn[:, 1, :], start=False, stop=True)
                    return fn
                S.op("pe", up(pm, h * 192), reads=["WQ", "cqn"], writes=[pmn])
                S.op("pe", up(pp, h * 192 + 96), reads=["WQ", "cqn"], writes=[ppn])
                S.op("act", lambda e, pm=pm, h=h, par=par: e.mul(out=qB_s[par][0:64, h, :], in_=pm[0:64, :], mul=SC96),
                     reads=[pmn], writes=["qB_s%d" % par])
                S.op("dve", lambda e, pm=pm, t1=t1: e.scalar_tensor_tensor(
                    out=t1[64:96, :], in0=pm[64:96, :], scalar=SC96, in1=tabs[64:96, 4, :], op0=ALU.mult, op1=ALU.mult),
                    reads=[pmn, "tabs"], writes=[t1n])
                S.op("dve", lambda e, pp=pp, t2=t2: e.scalar_tensor_tensor(
                    out=t2[64:96, :], in0=pp[64:96, :], scalar=SC96, in1=tabs[64:96, 5, :], op0=ALU.mult, op1=ALU.mult),
                    reads=[ppn, "tabs"], writes=[t2n])
                S.op("pool", lambda e, t1=t1, t2=t2, h=h, par=par: e.tensor_tensor(
                    out=qB_s[par][64:96, h, :], in0=t1[64:96, :], in1=t2[64:96, :], op=ALU.add),
                    reads=[t1n, t2n], writes=["qB_s%d" % par])
            pm, pmn = pj.next()
            sq, sqn = sqb.next()
            rs, rsn = rsb.next()
            S.op("pe", proj(pm, cBckv, 128, par), reads=rWc(cBckv), writes=[pmn])
            S.op("act", lambda e, pm=pm, sq=sq: e.activation(out=sq[:], in_=pm[:], func=AF.Square),
                 reads=[pmn], writes=[sqn])
            S.op("pe", lambda e, sq=sq: e.matmul(pn[:], lhsT=ones[:], rhs=sq[:], start=True, stop=True),
                 reads=[sqn, "ones"], writes=["pn"])
            S.op("act", lambda e, rs=rs: e.activation(out=rs[:], in_=pn[:], func=AF.Sqrt, bias=eps1[:], scale=1.0 / 128),
                 reads=["pn", "eps1"], writes=[rsn])
            S.op("dve", lambda e, rs=rs: e.reciprocal(out=rs[:], in_=rs[:]), reads=[rsn], writes=[rsn])
            S.op("dve", lambda e, pm=pm, rs=rs: e.scalar_tensor_tensor(
                out=ckvn[:], in0=pm[:], scalar=pc[:, 6:7], in1=rs[:], op0=ALU.mult, op1=ALU.mult),
                reads=[pmn, rsn, "pc"], writes=["ckvn"])
            for h in range(4):
                pu, pun = pj.next()
                S.op("pe", lambda e, pu=pu, h=h: e.matmul(pu[0:64, :], lhsT=WKV[:, h * 64:(h + 1) * 64], rhs=ckvn[:],
                                                          start=True, stop=True),
                     reads=["WKV", "ckvn"], writes=[pun])
                S.op("act", lambda e, pu=pu, h=h, par=par: e.copy(out=kB_s[par][0:64, h, :], in_=pu[0:64, :]),
                     reads=[pun], writes=["kB_s%d" % par])
            for j in range(4):
                S.op("pe", lambda e, j=j: e.matmul(pv[:, 0:256], lhsT=ckvn[:, j * 128:(j + 1) * 128], rhs=WKV[:, 256:512],
                                                   start=True, stop=True),
                     reads=["WKV", "ckvn"], writes=["pv"])
                S.op("act", lambda e, j=j, par=par: e.copy(out=vB_s[par][:, j, :], in_=pv[:, 0:256]),
                     reads=["pv"], writes=["vB_s%d" % par])
            pm, pmn = pj.next()
            pp, ppn = pj.next()
            t1, t1n = t1b.next()
            t2, t2n = t2b.next()
            S.op("pe", proj(pm, cBkr, 96, par), reads=rWc(cBkr), writes=[pmn])
            S.op("pe", proj(pp, cBkrp, 96, par), reads=rWc(cBkrp), writes=[ppn])
            S.op("dve", lambda e, pm=pm, t1=t1: e.tensor_tensor(out=t1[64:96, :], in0=pm[64:96, :], in1=tabs[64:96, 4, :],
                                                                op=ALU.mult), reads=[pmn, "tabs"], writes=[t1n])
            S.op("dve", lambda e, pp=pp, t2=t2: e.tensor_tensor(out=t2[64:96, :], in0=pp[64:96, :], in1=tabs[64:96, 5, :],
                                                                op=ALU.mult), reads=[ppn, "tabs"], writes=[t2n])
            for h in range(4):
                S.op("pool", lambda e, t1=t1, t2=t2, h=h, par=par: e.tensor_tensor(
                    out=kB_s[par][64:96, h, :], in0=t1[64:96, :], in1=t2[64:96, :], op=ALU.add),
                    reads=[t1n, t2n], writes=["kB_s%d" % par])
            for j in range(4):
                def vmm(e, j=j, par=par):
                    for c in range(8):
                        ins = e.matmul(pv[:], lhsT=hT[par][:, c, j * 128:(j + 1) * 128], rhs=W[:, c, cV:cV + 512],
                                       start=(c == 0), stop=(c == 7))
                    return ins
                S.op("pe", vmm, reads=rWc(cV), writes=["pv"])
                S.op("dve", lambda e, j=j, par=par: e.tensor_copy(out=V_s[par][:, j, :], in_=pv[:]),
                     reads=["pv"], writes=["V_s%d" % par])
            tsl = slice(tb * 512, (tb + 1) * 512)
            for (dr, src, nm, rr) in ((o_qA, qA_s, "qA_s", 128), (o_kA, kA_s, "kA_s", 128), (o_qC, qC_s, "qC_s", 128),
                                      (o_kC, kC_s, "kC_s", 128), (o_qD, qD_s, "qD_s", 128), (o_kD, kD_s, "kD_s", 128),
                                      (o_qB, qB_s, "qB_s", 96), (o_kB, kB_s, "kB_s", 96)):
                S.dma("sp", lambda e, dr=dr, src=src, rr=rr, par=par, tsl=tsl: e.dma_start(
                    out=dr.rearrange("(j p) t -> p j t", p=rr)[:, :, tsl], in_=src[par][:]),
                    reads=["%s%d" % (nm, par)], writes=["o_%s_%d" % (nm, tb)], semkey="D_st_%s%d" % (nm, par))
            S.dma("sp", lambda e, par=par, tb=tb: e.dma_start(
                out=o_V[tb * 512:(tb + 1) * 512, :].rearrange("(j p) n -> p j n", p=128), in_=V_s[par][:]),
                reads=["V_s%d" % par], writes=["o_V_%d" % tb], semkey="D_st_V%d" % par)
            S.dma("sp", lambda e, par=par, tb=tb: e.dma_start(
                out=o_vB[tb * 512:(tb + 1) * 512, :].rearrange("(j p) n -> p j n", p=128), in_=vB_s[par][:]),
                reads=["vB_s%d" % par], writes=["o_vB_%d" % tb], semkey="D_st_vB%d" % par)
        step1(0)
        for tb in range(NBLK):
            if tb + 1 < NBLK:
                step1(tb + 1)
            jobs(tb)
        S.emit()
    return nc


_SPLITS = (256, 256, 256, 256, 128, 32, 256, 128, 128, 256, 128, 128)
_OFF = np.concatenate([[0], np.cumsum(_SPLITS)])


def _perm_full():
    d = np.arange(64)
    return np.where(d < 32, d + 32, d - 32)


def _perm_ax():
    d = np.arange(64)
    return np.where((d % 32) < 16, d + 16, d - 16)


def _perm32():
    d = np.arange(32)
    return np.where(d < 16, d + 16, d - 16)


def _wext_index():
    o = _OFF
    rng = lambda i: np.arange(o[i], o[i + 1])

    def permheads(i, nh, perm):
        return np.concatenate([o[i] + h * 64 + perm for h in range(nh)])
    pad64 = -np.ones(64, np.int64)
    idx = np.concatenate([
        rng(0), rng(1),
        rng(6), permheads(6, 4, _perm_ax()), rng(7), permheads(7, 2, _perm_ax()),
        rng(9), permheads(9, 4, _perm_full()), rng(10), permheads(10, 2, _perm_full()),
        rng(3), rng(4),
        pad64, rng(5), pad64, o[5] + _perm32(),
        rng(2), rng(8), rng(11)])
    assert idx.shape[0] == NCX
    return idx


def _gather_cols(w, idx):
    out = np.zeros((w.shape[0], idx.shape[0]), w.dtype)
    m = idx >= 0
    out[:, m] = w[:, idx[m]]
    return out


def _rope_tables(r0):
    t = (r0 + np.arange(TCORE)).astype(np.float32)

    def ang(pos, dim):
        inv = (np.float32(10000.0) ** (-(np.arange(0, dim, 2, dtype=np.float32) / np.float32(dim)))).astype(np.float32)
        return (pos[:, None] * inv[None, :]).astype(np.float32)
    a_full = ang(t, 64)
    a_mla = ang(t, 32)
    ti = r0 + np.arange(TCORE)
    a_row = ang((ti // 64).astype(np.float32), 32)
    a_col = ang((ti % 64).astype(np.float32), 32)
    tabs = np.zeros((6, 128, TCORE), np.float32)
    for p in range(128):
        d = p % 64
        a = a_full[:, d % 32]
        tabs[0, p] = np.cos(a)
        tabs[1, p] = np.sin(a) * (-1.0 if d < 32 else 1.0)
        if d < 32:
            a = a_row[:, d % 16]
            sg = -1.0 if d < 16 else 1.0
        else:
            a = a_col[:, (d - 32) % 16]
            sg = -1.0 if (d - 32) < 16 else 1.0
        tabs[2, p] = np.cos(a)
        tabs[3, p] = np.sin(a) * sg
        if 64 <= p < 96:
            jj = p - 64
            a = a_mla[:, jj % 16]
            tabs[4, p] = np.cos(a)
            tabs[5, p] = np.sin(a) * (-1.0 if jj < 16 else 1.0)
    return tabs


_CACHE = {}
_TRACE_KW = {}


def _get(name, builder):
    if name not in _CACHE:
        _CACHE[name] = builder()
    return _CACHE[name]


def run_P(xfull, l, inp):
    nc = _get("P", build_P)
    widx = _get("widx", _wext_index)
    wext = _gather_cols(inp["w_in"][l], widx)
    wuq_src = inp["mla_w_uq"][l]
    cols = []
    for h in range(4):
        base = h * 96
        cols.append(np.arange(base, base + 96))
        cols.append(np.concatenate([-np.ones(64, np.int64), base + 64 + _perm32()]))
    wuq = _gather_cols(wuq_src, np.concatenate(cols))
    kvidx = np.concatenate([np.concatenate([np.arange(h * 128, h * 128 + 64) for h in range(4)]),
                            np.concatenate([np.arange(h * 128 + 64, h * 128 + 128) for h in range(4)])])
    wukv = _gather_cols(inp["mla_w_ukv"][l], kvidx)
    pa = _perm_ax()
    p64 = np.arange(128) % 64
    pcol = np.zeros((128, 8), np.float32)
    pcol[:, 0] = inp["ax_q_gain"][l][p64]
    pcol[:, 1] = inp["ax_q_gain"][l][pa[p64]]
    pcol[:, 2] = inp["ax_k_gain"][l][p64]
    pcol[:, 3] = inp["ax_k_gain"][l][pa[p64]]
    pcol[:, 4] = inp["mla_q_gain"][l][0:128]
    pcol[:, 5] = inp["mla_q_gain"][l][128:256]
    pcol[:, 6] = inp["mla_kv_gain"][l]
    gpre = np.ascontiguousarray(inp["mix_pre_gain"][l][None, :])
    in_maps = []
    for c in range(NCORES):
        b, r0 = c // 4, (c % 4) * TCORE
        tabs = _get("tab%d" % r0, lambda r0=r0: _rope_tables(r0))
        in_maps.append(dict(x=np.ascontiguousarray(xfull[b, r0:r0 + TCORE]), gpre=gpre, wext=wext, wuq=wuq,
                            wukv=wukv, tab=tabs, pcol=pcol))
    t0 = time.time()
    res = run_bass_kernel_spmd(nc, in_maps, core_ids=list(range(NCORES)), **_TRACE_KW)
    print("[kernel] launch took %.1fs" % (time.time() - t0), flush=True)
    return res.results


ARN = 23552
NWIN = 22
A_SLOT = [0, 1] + [2] * 12 + [3, 4]
D_SLOT = [0] + [1] * 14 + [2]
NEG = -1e30


def build_T():
    nc = bass.Bass("TRN2", target_bir_lowering=False)

    def din(name, shape, dt=F32):
        return nc.dram_tensor(name, shape, dt, kind="ExternalInput").ap()

    KD = din("KD", [4, 128, 2 * SEQ], BF16)
    VD = din("VD", [4, 128, 64 * 256], BF16)
    QD = din("QD", [4, 128, 2 * TCORE], BF16)
    kAw = din("kAw", [128, 2 * NWIN * 128], BF16)
    vAw = din("vAw", [128, NWIN * 256], BF16)
    kDw = din("kDw", [128, NWIN * 128], BF16)
    vDw = din("vDw", [128, NWIN * 128], BF16)
    qAd = din("qAd", [128, 4 * TCORE], BF16)
    qDd = din("qDd", [128, 4 * TCORE], BF16)
    tabA = din("tabA", [5, 128, 7 * 512], F32)
    tabD = din("tabD", [3, 128, 3 * 512], BF16)
    sinkc = din("sinkc", [128, 2], F32)
    xin = din("x", [TCORE, DM])
    wout = din("wout", [DM, DM])
    gpost = din("gpost", [1, DM])
    o_xm = nc.dram_tensor("xm", [TCORE, DM], F32, kind="ExternalOutput").ap()
    o_mix = nc.dram_tensor("mixT", [128, 8 * TCORE], BF16, kind="ExternalOutput").ap()

    with ExitStack() as st:
        def sb(n, s, d):
            return st.enter_context(nc.sbuf_tensor(n, s, d))

        def ps(n, s, d=F32):
            return st.enter_context(nc.psum_tensor(n, s, d))

        S = Sched(nc)
        AR = [sb("AR%d" % i, [128, ARN], BF16) for i in range(2)]
        mixT = sb("mixT_s", [128, 8, TCORE], BF16)
        VB = sb("VB", [128, 64, 2, 128], BF16)
        PT = Rot([(sb("PT%d" % i, [128, 512], BF16), "PT%d" % i) for i in range(8)])
        ones = sb("ones", [128, 64], BF16)
        rec = [sb("rec%d" % i, [128, 512], F32) for i in range(2)]
        sinke = sb("sinke", [128, 2], F32)
        gpo = sb("gpo", [128, DM], F32)
        eps1 = sb("eps1", [128, 1], F32)
        xt = [sb("xt%d" % i, [128, DM], F32) for i in range(2)]
        yt = [sb("yt%d" % i, [128, DM], F32) for i in range(2)]
        junk = sb("junk", [128, 512], BF16)
        ssq = [sb("ssq%d" % i, [128, 2], F32) for i in range(2)]
        rstd = [sb("rstd%d" % i, [128, 1], F32) for i in range(2)]
        psS = ps("psS", [128, 4, 512])
        acc = [(ps("acc%d_o" % i, [128, 512]), ps("acc%d_s" % i, [128, 512])) for i in range(2)]

        ident = _mk_consts(nc, S, sb)
        S.op("pool", lambda e: e.memset(ones[:], 1.0), writes=["ones"])
        S.op("pool", lambda e: e.memset(eps1[:], EPS), writes=["eps1"])
        S.dma("sp", lambda e: e.dma_start(out=sinke[:], in_=sinkc), writes=["sinke"])
        S.op("act", lambda e: e.activation(out=sinke[:], in_=sinke[:], func=AF.Exp), reads=["sinke"], writes=["sinke"])
        S.dma("sp", lambda e: e.dma_start(out=gpo[:], in_=gpost.partition_broadcast(128)), writes=["gpo"])

        def dense_views(s):
            a = AR[s]
            return (a[:, 0:16384].rearrange("p (u t) -> p u t", u=2), VB,
                    a[:, 16384:20480].rearrange("p (u t) -> p u t", u=2))

        def v_aug(s, kt, u):
            return VB[:, kt, u, :]

        def load_v(ph):
            S.dma("sp", lambda e, ph=ph: e.dma_start(
                out=VB[:], in_=VD[ph].rearrange("p (t u n) -> p t u n", u=2, n=128)), writes=["sV"], semkey="D_V")

        def load_dense(ph):
            s = ph % 2
            K, V, Q = dense_views(s)
            nm = ["s%dK" % s, "sV", "s%dQ" % s]
            for u in range(2):
                S.dma("sp", lambda e, K=K, ph=ph, u=u: e.dma_start(
                    out=K[:, u, :], in_=KD[ph][:, u * SEQ:(u + 1) * SEQ]), writes=[nm[0]], semkey="D_K%d" % s)
            S.dma("sp", lambda e, Q=Q, ph=ph: e.dma_start(
                out=Q[:, :, :], in_=QD[ph].rearrange("p (u t) -> p u t", u=2)), writes=[nm[2]], semkey="D_Q%d" % s)

        items = []
        NPH = int(os.environ.get("T_NPH", "4"))
        DO_AD = int(os.environ.get("T_AD", "1"))
        DO_WO = int(os.environ.get("T_WO", "1"))
        for ph in range(NPH):
            for qg in range(4):
                for kt in range(64):
                    for u in range(2):
                        items.append((ph, qg, kt, u))
        LOOK = 3
        N = len(items)
        pend = {}
        sbank = 0

        def load_ad():
            a = AR[0]
            al = ["s0K", "s0Q"]
            v = dict(kA=a[:, 0:5632], vA=a[:, 5632:11264], qA=a[:, 11264:19456])
            for nm, src in (("kA", kAw), ("vA", vAw), ("qA", qAd)):
                S.dma("sp", lambda e, dst=v[nm], src=src: e.dma_start(out=dst, in_=src),
                      writes=al + ["ad_" + nm], semkey="D_ad_" + nm)

        def load_d():
            a = AR[1]
            v = dict(kD=a[:, 8192:11008], vD=a[:, 11008:13824], qD=a[:, 13824:22016])
            for nm, src in (("kD", kDw), ("vD", vDw), ("qD", qDd)):
                S.dma("sp", lambda e, dst=v[nm], src=src: e.dma_start(out=dst, in_=src),
                      writes=["s1K", "s1Q", "ad_" + nm], semkey="D_ad_" + nm)

        def load_wo():
            WO = AR[1][:, 0:8192].rearrange("p (c n) -> p c n", c=8)
            wv = wout.rearrange("(c p) n -> p c n", p=128)
            for c in range(8):
                S.dma("pool", lambda e, c=c: e.dma_start(out=WO[:, c, :], in_=wv[:, c, :]),
                      writes=(["s1K", "s1Q", "WO"] if c == 0 else ["WO"]), semkey="D_WO")

        if not DO_AD or NPH < 4:
            S.op("pool", lambda e: e.memset(mixT[:], 0.0),
                 writes=(["mixT_%d_%d_%d" % (c, q, u) for c in range(2, 6) for q in range(4) for u in range(2)] +
                         ["mixT_%s_%d" % (k, i) for k in "AD" for i in range(16)]))
        if NPH > 0:
            load_dense(0)
            load_v(0)
        if NPH > 1:
            load_dense(1)
        if NPH < 4:
            load_ad()
        for n in range(N + LOOK):
            if n < N:
                ph, qg, kt, u = items[n]
                s = ph % 2
                K, V, Q = dense_views(s)
                kr = 128
                bk = sbank % 4
                sbank += 1
                pt, ptn = PT.next()
                S.op("pe", lambda e, K=K, Q=Q, kr=kr, bk=bk, kt=kt, u=u, qg=qg: e.matmul(
                    psS[:, bk, :], lhsT=K[0:kr, u, kt * 128:(kt + 1) * 128], rhs=Q[0:kr, u, qg * 512:(qg + 1) * 512],
                    start=True, stop=True), reads=["s%dK" % s, "s%dQ" % s], writes=["psS%d" % bk])
                S.op("act", lambda e, bk=bk, pt=pt: e.activation(out=pt[:], in_=psS[:, bk, :], func=AF.Exp),
                     reads=["psS%d" % bk], writes=[ptn])
                pend[n] = (pt, ptn)
            m = n - LOOK
            if m >= 0:
                ph, qg, kt, u = items[m]
                if qg == 0 and kt == 0 and u == 0 and ph >= 1:
                    load_v(ph)
                    if ph + 1 < NPH:
                        load_dense(ph + 1)
                    elif NPH == 4:
                        load_ad()
                s = ph % 2
                K, V, Q = dense_views(s)
                pt, ptn = pend.pop(m)
                aset = (ph * 4 + qg) % 2
                bank = acc[aset][u]
                S.op("pe", lambda e, s=s, kt=kt, u=u, pt=pt, bank=bank: e.matmul(
                    bank[:], lhsT=v_aug(s, kt, u), rhs=pt[:], start=(kt == 0), stop=(kt == 63)),
                    reads=["sV", ptn], writes=["acc%d_%d" % (aset, u)])
                if kt == 63:
                    chunk = 2 + ph
                    rc = rec[aset]
                    osl = slice(u * 64, (u + 1) * 64)
                    ssl = slice((1 - u) * 64, (2 - u) * 64)
                    S.op("dve", lambda e, rc=rc, bank=bank, osl=osl, ssl=ssl: e.reciprocal(out=rc[osl, :], in_=bank[ssl, :]),
                         reads=["acc%d_%d" % (aset, u)], writes=["rec%d_%d" % (aset, u)])
                    S.op("dve", lambda e, rc=rc, bank=bank, chunk=chunk, qg=qg, osl=osl: e.tensor_tensor(
                        out=mixT[osl, chunk, qg * 512:(qg + 1) * 512], in0=bank[osl, :], in1=rc[osl, :], op=ALU.mult),
                        reads=["acc%d_%d" % (aset, u), "rec%d_%d" % (aset, u)], writes=["mixT_%d_%d_%d" % (chunk, qg, u)])

        load_wo()
        load_d()
        a0, a1 = AR[0], AR[1]
        kA = a0[:, 0:5632].rearrange("p (c t) -> p c t", c=2)
        vA = a0[:, 5632:11264].rearrange("p (t n) -> p t n", n=256)
        qA = a0[:, 11264:19456].rearrange("p (h t) -> p h t", h=4)
        tA = a0[:, 19456:23040].rearrange("p (j n) -> p j n", j=7)
        kD = a1[:, 8192:11008]
        vD = a1[:, 11008:13824].rearrange("p (t n) -> p t n", n=128)
        qD = a1[:, 13824:22016].rearrange("p (h t) -> p h t", h=4)
        tD = a1[:, 22016:23552].rearrange("p (j n) -> p j n", j=3)
        for kind in ("A", "D"):
            cur = -1
            for i in range(16 if DO_AD else 0):
                slot = A_SLOT[i] if kind == "A" else D_SLOT[i]
                if slot != cur:
                    cur = slot
                    if kind == "A":
                        S.dma("pool", lambda e, cur=cur: e.dma_start(
                            out=tA, in_=tabA[cur].rearrange("p (j n) -> p j n", j=7)), writes=["ad_tA", "s0K", "s0Q"], semkey="D_tA")
                    else:
                        S.dma("sp", lambda e, cur=cur: e.dma_start(
                            out=tD, in_=tabD[cur].rearrange("p (j n) -> p j n", j=3)), writes=["ad_tD", "s1K", "s1Q"], semkey="D_tD")
                qs = slice(i * 128, (i + 1) * 128)
                nj = 7 if kind == "A" else 3
                aset = i % 2
                ao, asum = acc[aset]
                for j in range(nj):
                    bk = sbank % 4
                    sbank += 1
                    pt, ptn = PT.next()
                    kt = i + j if kind == "A" else i + 2 + j

                    def smm(e, kind=kind, bk=bk, j=j, kt=kt, qs=qs):
                        tt = tA if kind == "A" else tD
                        e.matmul(psS[:, bk, :], lhsT=ident[:], rhs=tt[:, j, :], start=True, stop=False)
                        for h in range(4):
                            if kind == "A":
                                lh = kA[:, h // 2, kt * 128:(kt + 1) * 128]
                                rh = qA[:, h, qs]
                            else:
                                lh = kD[:, kt * 128:(kt + 1) * 128]
                                rh = qD[:, h, qs]
                            ins = e.matmul(psS[:, bk, h * 128:(h + 1) * 128], lhsT=lh, rhs=rh, start=False, stop=(h == 3))
                        return ins
                    S.op("pe", smm, reads=(["ad_kA", "ad_qA", "ad_tA", "ident"] if kind == "A" else
                                           ["ad_kD", "ad_qD", "ad_tD", "ident"]), writes=["psS%d" % bk])
                    S.op("act", lambda e, bk=bk, pt=pt: e.activation(out=pt[:], in_=psS[:, bk, :], func=AF.Exp),
                         reads=["psS%d" % bk], writes=[ptn])

                    def pv(e, kind=kind, pt=pt, ao=ao, asum=asum, kt=kt, j=j, nj=nj):
                        for h in range(4):
                            r0, ch = (h % 2) * 64, h // 2
                            if kind == "A":
                                vv = vA[:, kt, h * 64:(h + 1) * 64]
                            else:
                                vv = vD[:, kt, ch * 64:(ch + 1) * 64]
                            st_ = (j == 0 and ch == 0)
                            e.matmul(ao[r0:r0 + 64, ch * 128:(ch + 1) * 128], lhsT=vv, rhs=pt[:, h * 128:(h + 1) * 128],
                                     start=st_, stop=(j == nj - 1), skip_group_check=True)
                            ins = e.matmul(asum[r0:r0 + 64, ch * 128:(ch + 1) * 128], lhsT=ones[:],
                                           rhs=pt[:, h * 128:(h + 1) * 128], start=st_, stop=(j == nj - 1),
                                           skip_group_check=True)
                        return ins
                    S.op("pe", pv, reads=["ad_vA" if kind == "A" else "ad_vD", ptn, "ones"],
                         writes=["acc%d_0" % aset, "acc%d_1" % aset])
                rc = rec[aset]
                ch0 = 0 if kind == "A" else 6
                if kind == "D":
                    for c in range(2):
                        S.op("dve", lambda e, rc=rc, asum=asum, c=c: e.tensor_scalar(
                            out=rc[:, c * 128:(c + 1) * 128], in0=asum[:, c * 128:(c + 1) * 128],
                            scalar1=sinke[:, c:c + 1], scalar2=None, op0=ALU.add),
                            reads=["acc%d_0" % aset, "acc%d_1" % aset, "sinke"], writes=["rec%d_0" % aset, "rec%d_1" % aset])
                    S.op("dve", lambda e, rc=rc: e.reciprocal(out=rc[:, 0:256], in_=rc[:, 0:256]),
                         reads=["rec%d_0" % aset, "rec%d_1" % aset], writes=["rec%d_0" % aset, "rec%d_1" % aset])
                else:
                    S.op("dve", lambda e, rc=rc, asum=asum: e.reciprocal(out=rc[:, 0:256], in_=asum[:, 0:256]),
                         reads=["acc%d_0" % aset, "acc%d_1" % aset], writes=["rec%d_0" % aset, "rec%d_1" % aset])
                S.op("dve", lambda e, rc=rc, ao=ao, ch0=ch0, qs=qs: e.tensor_tensor(
                    out=mixT[:, ch0:ch0 + 2, qs], in0=ao[:, 0:256].rearrange("p (c q) -> p c q", c=2),
                    in1=rc[:, 0:256].rearrange("p (c q) -> p c q", c=2), op=ALU.mult),
                    reads=["acc%d_0" % aset, "acc%d_1" % aset, "rec%d_0" % aset, "rec%d_1" % aset], writes=["mixT_%s_%d" % (kind, i)])

        mix_all = (["mixT_%d_%d_%d" % (c, q, u) for c in range(2, 6) for q in range(4) for u in range(2)] +
                   ["mixT_%s_%d" % (k, i) for k in "AD" for i in range(16)])
        S.dma("sp", lambda e: e.dma_start(out=o_mix.rearrange("p (c t) -> p c t", c=8), in_=mixT[:]),
              reads=mix_all, writes=["o_mix"])

        WO = AR[1][:, 0:8192].rearrange("p (c n) -> p c n", c=8)
        for i in range(16 if DO_WO else 0):
            b2 = i % 2
            hb = (i % 2) * 2
            S.dma("sp", lambda e, i=i, b2=b2: e.dma_start(out=xt[b2][:], in_=xin[i * 128:(i + 1) * 128, :]),
                  writes=["xt%d" % b2])

            def womm(e, i=i, hb=hb):
                for half in range(2):
                    for c in range(8):
                        ins = e.matmul(psS[:, hb + half, :], lhsT=mixT[:, c, i * 128:(i + 1) * 128],
                                       rhs=WO[:, c, half * 512:(half + 1) * 512], start=(c == 0), stop=(c == 7))
                return ins
            S.op("pe", womm, reads=mix_all + ["WO"], writes=["psS%d" % hb, "psS%d" % (hb + 1)])
            S.op("pool", lambda e, b2=b2: e.memset(ssq[b2][:], 0.0), writes=["ssq%d" % b2])
            for half in range(2):
                S.op("act", lambda e, b2=b2, hb=hb, half=half: e.activation(
                    out=junk[:], in_=psS[:, hb + half, :], func=AF.Square, accum_out=ssq[b2][:, half:half + 1]),
                    reads=["psS%d" % (hb + half), "ssq%d" % b2], writes=["junk", "ssq%d" % b2])
            S.op("dve", lambda e, b2=b2: e.tensor_tensor(out=rstd[b2][:], in0=ssq[b2][:, 0:1], in1=ssq[b2][:, 1:2],
                                                         op=ALU.add), reads=["ssq%d" % b2], writes=["rstd%d" % b2])
            S.op("act", lambda e, b2=b2: e.activation(out=rstd[b2][:], in_=rstd[b2][:], func=AF.Sqrt, bias=eps1[:],
                                                      scale=1.0 / DM), reads=["rstd%d" % b2, "eps1"], writes=["rstd%d" % b2])
            S.op("dve", lambda e, b2=b2: e.reciprocal(out=rstd[b2][:], in_=rstd[b2][:]),
                 reads=["rstd%d" % b2], writes=["rstd%d" % b2])
            for half in range(2):
                S.op("dve", lambda e, b2=b2, hb=hb, half=half: e.scalar_tensor_tensor(
                    out=yt[b2][:, half * 512:(half + 1) * 512], in0=psS[:, hb + half, :], scalar=rstd[b2][:, 0:1],
                    in1=gpo[:, half * 512:(half + 1) * 512], op0=ALU.mult, op1=ALU.mult),
                    reads=["psS%d" % (hb + half), "rstd%d" % b2, "gpo"], writes=["yt%d" % b2])
            S.op("pool", lambda e, b2=b2: e.tensor_tensor(out=yt[b2][:], in0=yt[b2][:], in1=xt[b2][:], op=ALU.add),
                 reads=["yt%d" % b2, "xt%d" % b2], writes=["yt%d" % b2])
            S.dma("sp", lambda e, i=i, b2=b2: e.dma_start(out=o_xm[i * 128:(i + 1) * 128, :], in_=yt[b2][:]),
                  reads=["yt%d" % b2], writes=["o_xm%d" % i], semkey="D_oxm%d" % b2)
        S.emit()
    return nc


def _a_table(rpb, gi):
    k = np.arange(128)
    q = np.arange(128)
    out = np.full((128, 7, 4, 128), NEG, np.float32)
    qq = gi * 128 + q
    qr, qc = qq // 64, qq % 64
    rs = np.clip(qr - 4, 0, 120)
    cs = np.clip(qc - 8, 0, 48)
    for j in range(7):
        kt = gi + j - 3
        if kt < 0 or kt >= 64:
            continue
        kk = kt * 128 + k
        kr_, kc = kk // 64, kk % 64
        valid = ((kr_[:, None] >= rs[None, :]) & (kr_[:, None] < rs[None, :] + 8) &
                 (kc[:, None] >= cs[None, :]) & (kc[:, None] < cs[None, :] + 16))
        ro = np.clip(kr_[:, None] - qr[None, :] + 7, 0, 14)
        co = np.clip(kc[:, None] - qc[None, :] + 15, 0, 30)
        for h in range(4):
            out[:, j, h, :] = np.where(valid, rpb[h][ro, co], np.float32(NEG))
    return out.reshape(128, 7 * 512)


def _d_table(gi):
    k = np.arange(128)
    q = np.arange(128)
    out = np.full((128, 3, 4, 128), NEG, np.float32)
    qpos = gi * 128 + q
    for j in range(3):
        kt = gi + j - 1
        if kt < 0 or kt >= 64:
            continue
        kpos = kt * 128 + k
        valid = np.abs(kpos[:, None] - qpos[None, :]) <= 128
        out[:, j, :, :] = np.where(valid, np.float32(0.0), np.float32(NEG))[:, None, :]
    return out.reshape(128, 3 * 512).astype(NPBF)


def _window(a, axis, start, length, total):
    lo, hi = max(start, 0), min(start + length, total)
    idx = [slice(None)] * a.ndim
    idx[axis] = slice(lo, hi)
    core = a[tuple(idx)]
    pads = [(0, 0)] * a.ndim
    pads[axis] = (lo - start, start + length - hi)
    return np.pad(core, pads)


def run_T(xfull, pres, l, inp):
    t0 = time.time()
    nc = _get("T", build_T)
    print("[kernel] build_T %.1fs" % (time.time() - t0), flush=True)
    rpb = inp["na_rpb"][l]
    wout = np.ascontiguousarray(inp["w_out"][l])
    gpost = np.ascontiguousarray(inp["mix_post_gain"][l][None, :])
    sink = inp["sw_sink"][l]
    p = np.arange(128)
    sinkc = np.stack([sink[2 * c + (p >= 64)] for c in range(2)], axis=1).astype(np.float32)
    in_maps = []
    for b in range(2):
        cores = [pres[b * 4 + i] for i in range(4)]
        cat = lambda k, ax: np.concatenate([np.asarray(cr[k]) for cr in cores], axis=ax)
        kB, kC, kA, kD = cat("kB", 1), cat("kC", 1), cat("kA", 1), cat("kD", 1)
        V, vB = cat("V", 0), cat("vB", 0)
        KDa = np.zeros((4, 128, 2 * SEQ), NPBF)
        VDa = np.ones((4, 128, 64, 2, 128), NPBF)
        for ph in range(4):
            if ph < 2:
                for u in range(2):
                    KDa[ph, 0:96, u * SEQ:(u + 1) * SEQ] = kB[(2 * ph + u) * 96:(2 * ph + u + 1) * 96]
                vs = vB[:, 2 * ph * 64:(2 * ph + 2) * 64]
            else:
                g = ph - 2
                for u in range(2):
                    KDa[ph, 0:64, u * SEQ:(u + 1) * SEQ] = kC[g * 64:(g + 1) * 64]
                vg = V[:, 256 + g * 64:256 + (g + 1) * 64]
                vs = np.concatenate([vg, vg], axis=1)
            vt = vs.reshape(64, 128, 128).transpose(1, 0, 2)
            VDa[ph, :, :, 0, 0:64] = vt[:, :, 0:64]
            VDa[ph, :, :, 1, 64:128] = vt[:, :, 64:128]
        for qd in range(4):
            c = b * 4 + qd
            r0 = qd * TCORE
            own = pres[c]
            QDa = np.zeros((4, 128, 2 * TCORE), NPBF)
            qB, qC = np.asarray(own["qB"]), np.asarray(own["qC"])
            for ph in range(4):
                for u in range(2):
                    if ph < 2:
                        QDa[ph, 0:96, u * TCORE:(u + 1) * TCORE] = qB[(2 * ph + u) * 96:(2 * ph + u + 1) * 96]
                    else:
                        h = 2 * (ph - 2) + u
                        QDa[ph, 0:64, u * TCORE:(u + 1) * TCORE] = qC[h * 64:(h + 1) * 64]
            w0, wl = r0 - 384, NWIN * 128
            kAw = _window(kA, 1, w0, wl, SEQ).reshape(2, 128, wl).transpose(1, 0, 2).reshape(128, 2 * wl)
            kDw = _window(kD, 1, w0, wl, SEQ)
            Vw = _window(V, 0, w0, wl, SEQ)
            vAw = Vw[:, 0:256].reshape(NWIN, 128, 256).transpose(1, 0, 2).reshape(128, NWIN * 256)
            vDw = Vw[:, 384:512].reshape(NWIN, 128, 128).transpose(1, 0, 2).reshape(128, NWIN * 128)
            qA_ = np.asarray(own["qA"])
            qD_ = np.asarray(own["qD"])
            qAd = np.zeros((128, 4, TCORE), NPBF)
            qDd = np.zeros((128, 4, TCORE), NPBF)
            for h in range(4):
                ra, rd = (h % 2) * 64, (h // 2) * 64
                qAd[ra:ra + 64, h] = qA_[h * 64:(h + 1) * 64]
                qDd[rd:rd + 64, h] = qD_[h * 64:(h + 1) * 64]
            qAd = qAd.reshape(128, 4 * TCORE)
            qDd = qDd.reshape(128, 4 * TCORE)
            g0 = qd * 16
            tabA = np.stack([_a_table(rpb, g0 + t) for t in (0, 1, 5, 14, 15)])
            tabD = _get("tabD%d" % qd, lambda g0=g0: np.stack([_d_table(g0 + t) for t in (0, 5, 15)]))
            in_maps.append(dict(KD=KDa, VD=VDa.reshape(4, 128, 64 * 256), QD=QDa, kAw=np.ascontiguousarray(kAw), vAw=np.ascontiguousarray(vAw),
                                kDw=np.ascontiguousarray(kDw), vDw=np.ascontiguousarray(vDw),
                                qAd=np.ascontiguousarray(qAd), qDd=np.ascontiguousarray(qDd), tabA=tabA, tabD=tabD,
                                sinkc=sinkc, x=np.ascontiguousarray(xfull[b, r0:r0 + TCORE]), wout=wout, gpost=gpost))
    t0 = time.time()
    res = run_bass_kernel_spmd(nc, in_maps, core_ids=list(range(NCORES)), **_TRACE_KW)
    print("[kernel] launch took %.1fs" % (time.time() - t0), flush=True)
    return res.results


NFC = DFF // 128
HW_ = TCORE + 2


def build_F():
    nc = bass.Bass("TRN2", target_bir_lowering=False)

    def din(name, shape, dt=F32):
        return nc.dram_tensor(name, shape, dt, kind="ExternalInput").ap()

    xm = din("xm", [TCORE, DM])
    xh = din("xh", [128, DM])
    g2 = din("g2", [1, DM])
    g3 = din("g3", [1, DM])
    wup = din("wup", [DM, NFC * 256])
    wdn = din("wdn", [DFF, DM])
    cpar = din("cpar", [128, NFC * 8])
    o_x = nc.dram_tensor("xo", [TCORE, DM], F32, kind="ExternalOutput").ap()

    with ExitStack() as st:
        def sb(n, s, d):
            return st.enter_context(nc.sbuf_tensor(n, s, d))

        def ps(n, s, d=F32):
            return st.enter_context(nc.psum_tensor(n, s, d))

        S = Sched(nc)
        h2T = sb("h2T", [128, 8, HW_], BF16)
        G = sb("G", [128, NFC, 1024], BF16)
        WD = sb("WD", [128, NFC, DM], BF16)
        WU = [sb("WU%d" % i, [128, 8, 256], BF16) for i in range(3)]
        ug = [sb("ug%d" % i, [128, 1026], F32) for i in range(2)]
        uv = [sb("uv%d" % i, [128, 1026], F32) for i in range(2)]
        cg = [sb("cg%d" % i, [128, 1024], F32) for i in range(2)]
        cv = [sb("cv%d" % i, [128, 1024], F32) for i in range(2)]
        xt = [sb("xt%d" % i, [128, DM], F32) for i in range(2)]
        yt = sb("yt", [128, DM], F32)
        hb = [sb("hb%d" % i, [128, DM], BF16) for i in range(2)]
        gp2 = sb("gp2", [128, DM], F32)
        gp3 = sb("gp3", [128, DM], F32)
        cp = sb("cp", [128, NFC, 2, 4], F32)
        eps1 = sb("eps1", [128, 1], F32)
        junk = sb("junk", [128, DM], BF16)
        ssq = [sb("ssq%d" % i, [128, 2], F32) for i in range(2)]
        rstd = [sb("rstd%d" % i, [128, 1], F32) for i in range(2)]
        pt = ps("pt", [128, DM], BF16)
        pu = Rot([(ps("pu%d" % i, [128, 512]), "pu%d" % i) for i in range(3)])
        py = [ps("py%d" % i, [128, 2, 512]) for i in range(2)]

        ident = _mk_consts(nc, S, sb)
        S.op("pool", lambda e: e.memset(eps1[:], EPS), writes=["eps1"])
        S.dma("sp", lambda e: e.dma_start(out=gp2[:], in_=g2.partition_broadcast(128)), writes=["gp2"])
        S.dma("sp", lambda e: e.dma_start(out=gp3[:], in_=g3.partition_broadcast(128)), writes=["gp3"])
        S.dma("sp", lambda e: e.dma_start(out=cp[:], in_=cpar.rearrange("p (f k w) -> p f k w", f=NFC, k=2)),
              writes=["cp"])

        h2names = []
        for i in range(17):
            b2 = i % 2
            src = xm[i * 128:(i + 1) * 128, :] if i < 16 else xh
            S.dma("sp", lambda e, src=src, b2=b2: e.dma_start(out=xt[b2][:], in_=src), writes=["xt%d" % b2])
            S.op("pool", lambda e, b2=b2: e.memset(ssq[b2][:], 0.0), writes=["ssq%d" % b2])
            S.op("act", lambda e, b2=b2: e.activation(out=junk[:], in_=xt[b2][:], func=AF.Square,
                                                      accum_out=ssq[b2][:, 0:1]),
                 reads=["xt%d" % b2, "ssq%d" % b2], writes=["junk", "ssq%d" % b2])
            S.op("act", lambda e, b2=b2: e.activation(out=rstd[b2][:], in_=ssq[b2][:, 0:1], func=AF.Sqrt,
                                                      bias=eps1[:], scale=1.0 / DM),
                 reads=["ssq%d" % b2, "eps1"], writes=["rstd%d" % b2])
            S.op("dve", lambda e, b2=b2: e.reciprocal(out=rstd[b2][:], in_=rstd[b2][:]),
                 reads=["rstd%d" % b2], writes=["rstd%d" % b2])
            S.op("dve", lambda e, b2=b2: e.scalar_tensor_tensor(
                out=hb[b2][:], in0=xt[b2][:], scalar=rstd[b2][:, 0:1], in1=gp2[:], op0=ALU.mult, op1=ALU.mult),
                reads=["xt%d" % b2, "rstd%d" % b2, "gp2"], writes=["hb%d" % b2])

            def tr(e, b2=b2):
                for c in range(8):
                    ins = e.transpose(out=pt[:, c * 128:(c + 1) * 128], in_=hb[b2][:, c * 128:(c + 1) * 128],
                                      identity=ident[:])
                return ins
            S.op("pe", tr, reads=["hb%d" % b2, "ident"], writes=["pt"])
            ptv = pt[:].rearrange("p (c t) -> p c t", c=8)
            nm = "h2T_%d" % i
            h2names.append(nm)
            if i < 16:
                S.op("act", lambda e, i=i, ptv=ptv: e.copy(out=h2T[:, :, 1 + i * 128:1 + (i + 1) * 128], in_=ptv),
                     reads=["pt"], writes=[nm])
            else:
                S.op("act", lambda e, ptv=ptv: e.copy(out=h2T[:, :, 0:HW_:HW_ - 1], in_=ptv[:, :, 0:2]),
                     reads=["pt"], writes=[nm])

        wdv = wdn.rearrange("(f p) n -> p f n", p=128)
        for f in range(NFC):
            S.dma("pool", lambda e, f=f: e.dma_start(out=WD[:, f, :], in_=wdv[:, f, :]), writes=["WD"], semkey="D_WD")

        wuv = wup.rearrange("(c p) n -> p c n", p=128)
        k = 0
        for hf in range(2):
            c0 = hf * 1024
            blocks = ((0, 512), (512, 512), (1024, 2))
            for fc in range(NFC):
                wb = k % 3
                ub = k % 2
                k += 1
                S.dma("pool", lambda e, wb=wb, fc=fc: e.dma_start(out=WU[wb][:], in_=wuv[:, :, fc * 256:(fc + 1) * 256]),
                      writes=["WU%d" % wb])
                for (kind, ubuf, unm, col) in (("g", ug, "ug", 0), ("v", uv, "uv", 128)):
                    for (b0, bn) in blocks:
                        p, pnm = pu.next()

                        def mm(e, p=p, wb=wb, col=col, b0=b0, bn=bn, c0=c0):
                            for c in range(8):
                                ins = e.matmul(p[:, 0:bn], lhsT=WU[wb][:, c, col:col + 128],
                                               rhs=h2T[:, c, c0 + b0:c0 + b0 + bn], start=(c == 0), stop=(c == 7))
                            return ins
                        S.op("pe", mm, reads=["WU%d" % wb] + h2names, writes=[pnm])
                        S.op("act", lambda e, p=p, ubuf=ubuf, ub=ub, b0=b0, bn=bn: e.copy(
                            out=ubuf[ub][:, b0:b0 + bn], in_=p[:, 0:bn]), reads=[pnm], writes=["%s%d" % (unm, ub)])
                for (ubuf, unm, cbuf, cnm, kk) in ((ug, "ug", cg, "cg", 0), (uv, "uv", cv, "cv", 1)):
                    u_, c_ = ubuf[ub], cbuf[ub]
                    if kk == 0:
                        S.op("dve", lambda e, u_=u_, c_=c_, fc=fc, kk=kk: e.tensor_scalar(
                            out=c_[:], in0=u_[:, 1:1025], scalar1=cp[:, fc, kk, 1:2], scalar2=cp[:, fc, kk, 3:4],
                            op0=ALU.mult, op1=ALU.add), reads=["%s%d" % (unm, ub), "cp"], writes=["%s%d" % (cnm, ub)])
                    else:
                        S.op("act", lambda e, u_=u_, c_=c_, fc=fc, kk=kk: e.activation(
                            out=c_[:], in_=u_[:, 1:1025], func=AF.Identity, scale=cp[:, fc, kk, 1:2],
                            bias=cp[:, fc, kk, 3:4]), reads=["%s%d" % (unm, ub), "cp"], writes=["%s%d" % (cnm, ub)])
                    S.op("dve", lambda e, u_=u_, c_=c_, fc=fc, kk=kk: e.scalar_tensor_tensor(
                        out=c_[:], in0=u_[:, 0:1024], scalar=cp[:, fc, kk, 0:1], in1=c_[:], op0=ALU.mult, op1=ALU.add),
                        reads=["%s%d" % (unm, ub), "%s%d" % (cnm, ub), "cp"], writes=["%s%d" % (cnm, ub)])
                    S.op("dve", lambda e, u_=u_, c_=c_, fc=fc, kk=kk: e.scalar_tensor_tensor(
                        out=c_[:], in0=u_[:, 2:1026], scalar=cp[:, fc, kk, 2:3], in1=c_[:], op0=ALU.mult, op1=ALU.add),
                        reads=["%s%d" % (unm, ub), "%s%d" % (cnm, ub), "cp"], writes=["%s%d" % (cnm, ub)])
                S.op("act", lambda e, ub=ub: e.activation(out=cg[ub][:], in_=cg[ub][:], func=AF.Gelu_apprx_tanh),
                     reads=["cg%d" % ub], writes=["cg%d" % ub])
                S.op("dve", lambda e, ub=ub, fc=fc: e.tensor_tensor(out=G[:, fc, :], in0=cg[ub][:], in1=cv[ub][:],
                                                                    op=ALU.mult),
                     reads=["cg%d" % ub, "cv%d" % ub], writes=["G_%d" % fc])
            gnames = ["G_%d" % f for f in range(NFC)]
            for il in range(8):
                i = hf * 8 + il
                b2 = i % 2
                pyb = py[b2]
                S.dma("sp", lambda e, i=i, b2=b2: e.dma_start(out=xt[b2][:], in_=xm[i * 128:(i + 1) * 128, :]),
                      writes=["xt%d" % b2])

                def dmm(e, il=il, pyb=pyb):
                    for half in range(2):
                        for f in range(NFC):
                            ins = e.matmul(pyb[:, half, :], lhsT=G[:, f, il * 128:(il + 1) * 128],
                                           rhs=WD[:, f, half * 512:(half + 1) * 512], start=(f == 0), stop=(f == NFC - 1))
                    return ins
                S.op("pe", dmm, reads=gnames + ["WD"], writes=["py%d" % b2])
                S.op("pool", lambda e, b2=b2: e.memset(ssq[b2][:], 0.0), writes=["ssq%d" % b2])
                for half in range(2):
                    S.op("act", lambda e, b2=b2, pyb=pyb, half=half: e.activation(
                        out=junk[:, 0:512], in_=pyb[:, half, :], func=AF.Square, accum_out=ssq[b2][:, half:half + 1]),
                        reads=["py%d" % b2, "ssq%d" % b2], writes=["junk", "ssq%d" % b2])
                S.op("dve", lambda e, b2=b2: e.tensor_tensor(out=rstd[b2][:], in0=ssq[b2][:, 0:1], in1=ssq[b2][:, 1:2],
                                                             op=ALU.add), reads=["ssq%d" % b2], writes=["rstd%d" % b2])
                S.op("act", lambda e, b2=b2: e.activation(out=rstd[b2][:], in_=rstd[b2][:], func=AF.Sqrt, bias=eps1[:],
                                                          scale=1.0 / DM), reads=["rstd%d" % b2, "eps1"], writes=["rstd%d" % b2])
                S.op("dve", lambda e, b2=b2: e.reciprocal(out=rstd[b2][:], in_=rstd[b2][:]),
                     reads=["rstd%d" % b2], writes=["rstd%d" % b2])
                for half in range(2):
                    S.op("dve", lambda e, b2=b2, pyb=pyb, half=half: e.scalar_tensor_tensor(
                        out=yt[:, half * 512:(half + 1) * 512], in0=pyb[:, half, :], scalar=rstd[b2][:, 0:1],
                        in1=gp3[:, half * 512:(half + 1) * 512], op0=ALU.mult, op1=ALU.mult),
                        reads=["py%d" % b2, "rstd%d" % b2, "gp3"], writes=["yt"])
                S.op("pool", lambda e, b2=b2: e.tensor_tensor(out=yt[:], in0=yt[:], in1=xt[b2][:], op=ALU.add),
                     reads=["yt", "xt%d" % b2], writes=["yt"])
                S.dma("sp", lambda e, i=i: e.dma_start(out=o_x[i * 128:(i + 1) * 128, :], in_=yt[:]),
                      reads=["yt"], writes=["o_x%d" % i], semkey="D_ox")
        S.emit()
    return nc


def run_F(xm_full, l, inp):
    nc = _get("F", build_F)
    wu = inp["w_up"][l]
    idx = np.concatenate([np.concatenate([np.arange(f * 128, (f + 1) * 128), DFF + np.arange(f * 128, (f + 1) * 128)])
                          for f in range(NFC)])
    wup = np.ascontiguousarray(wu[:, idx])
    wdn = np.ascontiguousarray(inp["w_down"][l])
    cw, cb = inp["conv_w"][l], inp["conv_b"][l]
    cpar = np.zeros((128, NFC, 2, 4), np.float32)
    for k in range(2):
        for w in range(3):
            cpar[:, :, k, w] = cw[w, k * DFF:(k + 1) * DFF].reshape(NFC, 128).T
        cpar[:, :, k, 3] = cb[k * DFF:(k + 1) * DFF].reshape(NFC, 128).T
    cpar = cpar.reshape(128, NFC * 8)
    g2 = np.ascontiguousarray(inp["ffn_pre_gain"][l][None, :])
    g3 = np.ascontiguousarray(inp["ffn_post_gain"][l][None, :])
    in_maps = []
    for c in range(NCORES):
        b, r0 = c // 4, (c % 4) * TCORE
        xh = np.zeros((128, DM), np.float32)
        if r0 > 0:
            xh[0] = xm_full[b, r0 - 1]
        if r0 + TCORE < SEQ:
            xh[1] = xm_full[b, r0 + TCORE]
        in_maps.append(dict(xm=np.ascontiguousarray(xm_full[b, r0:r0 + TCORE]), xh=xh, g2=g2, g3=g3, wup=wup,
                            wdn=wdn, cpar=cpar))
    t0 = time.time()
    res = run_bass_kernel_spmd(nc, in_maps, core_ids=list(range(NCORES)), **_TRACE_KW)
    print("[kernel] launch took %.1fs" % (time.time() - t0), flush=True)
    out = np.zeros((2, SEQ, DM), np.float32)
    for c in range(NCORES):
        b, r0 = c // 4, (c % 4) * TCORE
        out[b, r0:r0 + TCORE] = np.asarray(res.results[c]["xo"])
    return out


def kernel(**inputs):
    inp = {k: np.asarray(v) for k, v in inputs.items()}
    x = np.ascontiguousarray(inp["x"], dtype=np.float32)
    for l in range(2):
        pres = run_P(x, l, inp)
        tres = run_T(x, pres, l, inp)
        xm = np.zeros((2, SEQ, DM), np.float32)
        for c in range(NCORES):
            b, r0 = c // 4, (c % 4) * TCORE
            xm[b, r0:r0 + TCORE] = np.asarray(tres[c]["xm"])
        x = run_F(xm, l, inp)
    return x
```
